# Optimizing a Trainium2 kernel written in Bass

```python
import math
import jax, jax.numpy as jnp
from jax import lax
import numpy as np

D_MODEL = 1024
BATCH = 16
SEQ = 4096
DEPTH = 1

POOL_GROUPS = 4
POOL_WINDOWS = (2, 4, 8, 16)
POOL_WIDTH = D_MODEL // 2
POOL_GROUP_DIM = POOL_WIDTH // POOL_GROUPS
N_HEADS = 8
QK_NOPE_DIM = 128
QK_ROPE_DIM = 64
V_DIM = 128
Q_LORA = D_MODEL // 4
KV_LORA = D_MODEL // 8
QK_DIM = QK_NOPE_DIM + QK_ROPE_DIM
ROPE_THETA = 10000.0
Q_BLOCK = 128
D_FF = 4 * D_MODEL
EPS = 1e-6
IN_COLS = POOL_WIDTH + Q_LORA + KV_LORA + QK_ROPE_DIM + 2 * D_MODEL
SPLITS = tuple(np.cumsum([POOL_WIDTH, Q_LORA, KV_LORA, QK_ROPE_DIM, D_MODEL]).tolist())

kernel_name = "hybrid_pool_mla_gated_block"


def rmsnorm(x, g):
    xf = x.astype(jnp.float32)
    xf = xf * lax.rsqrt(jnp.mean(xf * xf, axis=-1, keepdims=True) + EPS)
    return (xf * g.astype(jnp.float32)).astype(x.dtype)


def rope(x, cos, sin):
    x1, x2 = jnp.split(x, 2, axis=-1)
    return jnp.concatenate([x1 * cos - x2 * sin, x2 * cos + x1 * sin], axis=-1)


def causal_window_mean(u, w):
    s = u.shape[1]
    cs = jnp.cumsum(u, axis=1)
    lag = jnp.pad(cs[:, : s - w], ((0, 0), (w, 0), (0, 0)))
    count = jnp.minimum(jnp.arange(1, s + 1), w).astype(jnp.float32)
    return (cs - lag) / count[None, :, None]


def pool_mixer(u, pool_w, pool_scale, w_proj):
    b, s, _ = u.shape
    ug = u.reshape(b, s, POOL_GROUPS, POOL_GROUP_DIM)
    pooled = []
    for g, w in enumerate(POOL_WINDOWS):
        ui = ug[:, :, g].astype(jnp.float32)
        pooled.append(causal_window_mean(ui, w) - ui)
    pooled = jnp.stack(pooled, axis=2).astype(u.dtype)
    mixed = jnp.einsum('bsgc,gcd->bsgd', pooled, pool_w).reshape(b, s, POOL_WIDTH)
    return (mixed * pool_scale) @ w_proj


def mla_mixer(c_q, c_kv, k_rope_in, cos, sin, g_q, w_uq, g_kv, w_ukv, w_proj):
    b, s, _ = c_q.shape
    q = (rmsnorm(c_q, g_q) @ w_uq).reshape(b, s, N_HEADS, QK_DIM)
    q_nope, q_rope = q[..., :QK_NOPE_DIM], q[..., QK_NOPE_DIM:]
    q_rope = rope(q_rope, cos[:, :, None, :], sin[:, :, None, :])
    kv = (rmsnorm(c_kv, g_kv) @ w_ukv).reshape(b, s, N_HEADS, QK_NOPE_DIM + V_DIM)
    k_nope, v = kv[..., :QK_NOPE_DIM], kv[..., QK_NOPE_DIM:]
    k_rope = rope(k_rope_in, cos, sin)
    scale = 1.0 / math.sqrt(QK_DIM)
    k_idx = jnp.arange(s)

    def block(i):
        start = i * Q_BLOCK
        qn = lax.dynamic_slice_in_dim(q_nope, start, Q_BLOCK, axis=1)
        qr = lax.dynamic_slice_in_dim(q_rope, start, Q_BLOCK, axis=1)
        sc = (jnp.einsum('bqhd,bkhd->bhqk', qn, k_nope)
              + jnp.einsum('bqhd,bkd->bhqk', qr, k_rope)).astype(jnp.float32) * scale
        q_idx = start + jnp.arange(Q_BLOCK)
        mask = k_idx[None, :] <= q_idx[:, None]
        p = jax.nn.softmax(jnp.where(mask, sc, -jnp.inf), axis=-1).astype(v.dtype)
        return jnp.einsum('bhqk,bkhd->bqhd', p, v)

    o = lax.map(block, jnp.arange(s // Q_BLOCK))
    o = jnp.transpose(o, (1, 0, 2, 3, 4)).reshape(b, s, N_HEADS * V_DIM)
    return o @ w_proj


def setup_inputs(seed: int = 0) -> dict:
    key = jax.random.key(seed)
    ks = jax.random.split(key, 24)
    L, D = DEPTH, D_MODEL

    def w(k, shape, fan_in):
        return jax.random.normal(k, shape, jnp.float32) * fan_in ** -0.5

    def gain(k, shape):
        return 1.0 + 0.01 * jax.random.normal(k, shape, jnp.float32)

    x = jax.random.normal(ks[0], (BATCH, SEQ, D), jnp.float32)
    offsets = jax.random.randint(ks[1], (BATCH, 1), 0, 1024, jnp.int32)
    positions = (jnp.arange(SEQ, dtype=jnp.int32)[None, :] + offsets).astype(jnp.int32)
    return {
        "x": x,
        "positions": positions,
        "g_mix": gain(ks[2], (L, D)),
        "w_in": w(ks[3], (L, D, IN_COLS), D),
        "b_gate": 0.01 * jax.random.normal(ks[4], (L, 2 * D), jnp.float32),
        "pool_w": w(ks[5], (L, POOL_GROUPS, POOL_GROUP_DIM, POOL_GROUP_DIM), POOL_GROUP_DIM),
        "pool_scale": gain(ks[6], (L, POOL_WIDTH)),
        "w_pool_proj": w(ks[7], (L, POOL_WIDTH, D), POOL_WIDTH),
        "g_q": gain(ks[8], (L, Q_LORA)),
        "w_uq": w(ks[9], (L, Q_LORA, N_HEADS * QK_DIM), Q_LORA),
        "g_kv": gain(ks[10], (L, KV_LORA)),
        "w_ukv": w(ks[11], (L, KV_LORA, N_HEADS * (QK_NOPE_DIM + V_DIM)), KV_LORA),
        "w_attn_proj": w(ks[12], (L, N_HEADS * V_DIM, D), N_HEADS * V_DIM),
        "w_out": w(ks[13], (L, D, D), D),
        "g_mlp": gain(ks[14], (L, D)),
        "w_mlp_in": w(ks[15], (L, D, D_FF), D),
        "w_mlp_out": w(ks[16], (L, D_FF, D), D_FF),
        "g_final": gain(ks[17], (D,)),
    }


def reference(x, positions, g_mix, w_in, b_gate, pool_w, pool_scale, w_pool_proj,
              g_q, w_uq, g_kv, w_ukv, w_attn_proj, w_out, g_mlp, w_mlp_in,
              w_mlp_out, g_final):
    half = QK_ROPE_DIM // 2
    inv_freq = ROPE_THETA ** (-jnp.arange(half, dtype=jnp.float32) / half)
    ang = positions.astype(jnp.float32)[..., None] * inv_freq
    cos = jnp.cos(ang).astype(x.dtype)
    sin = jnp.sin(ang).astype(x.dtype)

    for l in range(DEPTH):
        h = rmsnorm(x, g_mix[l])
        z = h @ w_in[l]
        u_pool, c_q, c_kv, k_rope_in, gate_a, gate_b = jnp.split(z, SPLITS, axis=-1)
        gate_a = jax.nn.sigmoid(gate_a + b_gate[l, :D_MODEL])
        gate_b = jax.nn.sigmoid(gate_b + b_gate[l, D_MODEL:])
        a = pool_mixer(u_pool, pool_w[l], pool_scale[l], w_pool_proj[l])
        bm = mla_mixer(c_q, c_kv, k_rope_in, cos, sin, g_q[l], w_uq[l], g_kv[l],
                       w_ukv[l], w_attn_proj[l])
        x = x + (gate_a * a + gate_b * bm) @ w_out[l]
        hm = rmsnorm(x, g_mlp[l])
        x = x + jnp.square(jax.nn.relu(hm @ w_mlp_in[l])) @ w_mlp_out[l]

    return rmsnorm(x, g_final)
```

```python
import math
import numpy as np
import concourse.bass as bass
import concourse.mybir as mybir
from concourse.bass_utils import run_bass_kernel_spmd

F32 = mybir.dt.float32
BF16 = mybir.dt.bfloat16
I32 = mybir.dt.int32
U8 = mybir.dt.uint8
ALU = mybir.AluOpType
AF = mybir.ActivationFunctionType

D = 1024
T = 512
NT = 4
NH = 8
DFF = 4096
EPS = 1e-6
SCALE = 1.0 / math.sqrt(192.0)
NSLOT = 5
PAGE = 1024
N_CORES = 8

C_GMIX, C_GMLP, C_PSC, C_GQ, C_GKV, C_BG, C_INVF, C_SGN, C_EPS, C_QUART, NCOL = 0, 8, 16, 20, 22, 23, 39, 40, 41, 42, 43


def _flat(keys):
    out = []
    for k in keys:
        if isinstance(k, list):
            out.extend(_flat(k))
        else:
            out.append(k)
    return out


class Sched:
    def __init__(self, nc):
        self.nc = nc
        self.eng = {"pe": nc.tensor, "dve": nc.vector, "act": nc.scalar, "pool": nc.gpsimd, "sp": nc.sync}
        self.semobj = {}
        self._ctx = []
        self.cnt = {}
        for k in self.eng:
            self.new_sem(k)
            self.cnt[k] = 0
        self.waited = {k: {} for k in self.eng}
        self.lastw = {}
        self.readers = {}
        self.dcnt = {}
        self.phase = "init"
        self.pe_labels = []

    def new_sem(self, name):
        cm = self.nc.semaphore("s_" + name)
        self.semobj[name] = cm.__enter__()
        self._ctx.append(cm)
        return name

    def close(self):
        for cm in reversed(self._ctx):
            cm.__exit__(None, None, None)

    def _deps(self, reads, writes):
        toks = []
        for k in reads:
            t = self.lastw.get(k)
            if t is not None:
                toks.append(t)
        for k in writes:
            t = self.lastw.get(k)
            if t is not None:
                toks.append(t)
            toks.extend(self.readers.get(k, ()))
        return toks

    def _pending(self, e, toks):
        need = {}
        for (s, v) in toks:
            if v > need.get(s, 0):
                need[s] = v
        w = self.waited[e]
        pend = []
        for s, v in need.items():
            if w.get(s, 0) < v:
                pend.append((s, v))
                w[s] = v
        return pend

    def _wait(self, e, toks):
        for s, v in self._pending(e, toks):
            self.eng[e].wait_ge(self.semobj[s], v)

    def _record(self, tok, reads, writes):
        for k in reads:
            self.readers.setdefault(k, []).append(tok)
        for k in writes:
            self.lastw[k] = tok
            self.readers[k] = []

    def op(self, e, fn, reads=(), writes=()):
        return self.group(e, [fn], reads, writes)

    def group(self, e, fns, reads=(), writes=()):
        if e == "pe":
            self.pe_labels.extend([self.phase] * len(fns))
        reads, writes = _flat(list(reads)), _flat(list(writes))
        pend = self._pending(e, self._deps(reads, writes))
        emb = pend.pop() if pend else None
        for s, v in pend:
            self.eng[e].wait_ge(self.semobj[s], v)
        ins = None
        for n, fn in enumerate(fns):
            ins = fn(self.eng[e])
            if n == 0 and emb is not None:
                ins._wait_ge(self.semobj[emb[0]], emb[1])
        self.cnt[e] += 1
        ins.then_inc(self.semobj[e], 1)
        tok = (e, self.cnt[e])
        self._record(tok, reads, writes)
        return tok

    def dma(self, e, semname, out, in_, reads=(), writes=()):
        reads, writes = _flat(list(reads)), _flat(list(writes))
        self._wait(e, self._deps(reads, writes))
        self.eng[e].dma_start(out=out, in_=in_).then_inc(self.semobj[semname], 16)
        self.dcnt[semname] = self.dcnt.get(semname, 0) + 16
        tok = (semname, self.dcnt[semname])
        self._record(tok, reads, writes)
        return tok

    def wait_keys(self, e, keys):
        keys = _flat(list(keys))
        toks = []
        for k in keys:
            t = self.lastw.get(k)
            if t is not None:
                toks.append(t)
            toks.extend(self.readers.get(k, ()))
        self._wait(e, toks)


def pg(off, nbytes):
    return [("pg", i) for i in range(off // PAGE, (off + nbytes - 1) // PAGE + 1)]


def piece_list():
    P = [("z", i) for i in range(5)]
    for mc in range(8):
        P.append(("g", mc))
    for q in range(4):
        P.append(("wvo", q))
    for nh in range(2):
        for r in range(2):
            P.append(("wo", nh, r))
    for q in range(16):
        P.append(("w1", q))
    for nh in range(2):
        for r in range(8):
            P.append(("w2", nh, r))
    return P


ZSETS = [[4, 5], [6, 7], [8], [0, 1], [2, 3]]
PIECES = piece_list()
NPIECE = len(PIECES)
HOST_PIECES = [i for i, p in enumerate(PIECES) if p[0] != "wvo"]
WVO_IDX = {p[1]: i for i, p in enumerate(PIECES) if p[0] == "wvo"}


def lhs_piece(W, colsets):
    K = W.shape[0]
    kc = K // 128
    out = np.zeros((128, 2048), np.float32)
    blocks = [W[:, cs].reshape(kc, 128, 128).transpose(1, 0, 2) for cs in colsets]
    arr = np.stack(blocks, axis=1).reshape(128, -1)
    out[:, : arr.shape[1]] = arr
    return out


def rhs_piece(W, rows0, col0):
    blk = W[rows0: rows0 + 512, col0: col0 + 512].reshape(4, 128, 512).transpose(1, 0, 2)
    return np.ascontiguousarray(blk).reshape(128, 2048)


def host_prep(inp):
    f = lambda a: np.asarray(a, np.float32)
    w_in = f(inp["w_in"])[0]
    ar = np.arange
    zc = [ar(0, 128), ar(128, 256), ar(256, 384), ar(384, 512),
          ar(512, 640), ar(640, 768),
          ar(768, 896),
          np.concatenate([ar(896, 960), ar(896, 960)]),
          np.concatenate([ar(928, 960), ar(896, 928), ar(928, 960), ar(896, 928)])]
    ga = [ar(960 + 128 * m, 960 + 128 * (m + 1)) for m in range(8)]
    gb = [ar(1984 + 128 * m, 1984 + 128 * (m + 1)) for m in range(8)]
    wpp = f(inp["w_pool_proj"])[0]
    w_out = f(inp["w_out"])[0]
    w1 = f(inp["w_mlp_in"])[0]
    w2 = f(inp["w_mlp_out"])[0]
    pieces = []
    for i in HOST_PIECES:
        p = PIECES[i]
        if p[0] == "z":
            pieces.append(lhs_piece(w_in, [zc[i] for i in ZSETS[p[1]]]))
        elif p[0] == "g":
            pieces.append(lhs_piece(w_in, [ga[p[1]], gb[p[1]]]))
        elif p[0] == "wo":
            pieces.append(rhs_piece(w_out, 512 * p[2], 512 * p[1]))
        elif p[0] == "w1":
            pieces.append(lhs_piece(w1, [ar(128 * (2 * p[1] + j), 128 * (2 * p[1] + j + 1)) for j in range(2)]))
        elif p[0] == "w2":
            pieces.append(rhs_piece(w2, 512 * p[2], 512 * p[1]))
    wst = np.stack(pieces, axis=0)
    wpph = np.concatenate([lhs_piece(wpp, [ar(128 * (4 * q + j), 128 * (4 * q + j + 1)) for j in range(4)])
                           for q in range(2)], axis=1)

    w_uq = f(inp["w_uq"])[0]
    w_ukv = f(inp["w_ukv"])[0]
    w_ap = f(inp["w_attn_proj"])[0]
    rope_cols, swap_cols = [], []
    for j in range(4):
        c, s = [], []
        for h in (2 * j, 2 * j + 1):
            b = h * 192 + 128
            c.append(ar(b, b + 64))
            s.append(np.concatenate([ar(b + 32, b + 64), ar(b, b + 32)]))
        rope_cols.append(np.concatenate(c))
        swap_cols.append(np.concatenate(s))
    wqr = lhs_piece(w_uq, rope_cols + swap_cols)
    A1 = np.stack([w_uq[:, h * 192: h * 192 + 128].T for h in range(8)], axis=1).reshape(128, 2048)
    A2 = np.stack([w_ukv[:, h * 256: h * 256 + 128].T for h in range(8)], axis=1).reshape(128, 1024)
    A3 = np.stack([w_ukv[:, h * 256 + 128: h * 256 + 256].T for h in range(8)], axis=1).reshape(128, 1024)
    A4 = np.ascontiguousarray(w_ap.reshape(8, 128, 1024).transpose(1, 0, 2)).reshape(128, 8192)
    poolw = np.ascontiguousarray(f(inp["pool_w"])[0].transpose(1, 0, 2)).reshape(128, 512)

    cols = np.zeros((128, NCOL), np.float32)
    cols[:, C_GMIX:C_GMIX + 8] = f(inp["g_mix"])[0].reshape(8, 128).T
    cols[:, C_GMLP:C_GMLP + 8] = f(inp["g_mlp"])[0].reshape(8, 128).T
    cols[:, C_PSC:C_PSC + 4] = f(inp["pool_scale"])[0].reshape(4, 128).T
    cols[:, C_GQ:C_GQ + 2] = f(inp["g_q"])[0].reshape(2, 128).T
    cols[:, C_GKV] = f(inp["g_kv"])[0]
    cols[:, C_BG:C_BG + 16] = f(inp["b_gate"])[0].reshape(16, 128).T
    half = 32
    inv_freq = (10000.0 ** (-np.arange(half, dtype=np.float32) / half)).astype(np.float32)
    p = np.arange(128)
    cols[:, C_INVF] = (inv_freq[p % 32].astype(np.float64) / (2 * np.pi)).astype(np.float32)
    twopi = 2 * np.pi * (1 - 1e-6)
    cols[:, C_SGN] = np.where((p % 64) < 32, -twopi, twopi).astype(np.float32)
    cols[:, C_EPS] = EPS
    cols[:, C_QUART] = 0.25
    ident = np.eye(128, dtype=np.float32)
    tri = (np.arange(128)[None, :] >= np.arange(128)[:, None]).astype(np.float32)
    invcnt = np.tile((1.0 / np.arange(1, 17, dtype=np.float32))[None, :], (128, 1)).astype(np.float32)
    return dict(wst=wst, wqr=wqr, A1=A1, A2=A2, A3=A3, A4=A4, poolw=poolw, cols=cols, ident=ident, tri=tri,
                invcnt=invcnt, wpph=wpph, gfin=f(inp["g_final"]).reshape(1, 1024))


def build(nseq=2, nch=8):
    S_LEN = nch * T
    NTOK = nseq * S_LEN
    NCHUNK = nseq * nch
    nc = bass.Bass("TRN2", target_bir_lowering=False)
    x_d = nc.dram_tensor("x", [NTOK, D], F32, kind="ExternalInput").ap()
    pos_d = nc.dram_tensor("pos", [1, NTOK], I32, kind="ExternalInput").ap()
    wst_d = nc.dram_tensor("wst", [len(HOST_PIECES), 128, 2048], F32, kind="ExternalInput").ap()
    wqr_d = nc.dram_tensor("wqr", [128, 2048], F32, kind="ExternalInput").ap()
    A1_d = nc.dram_tensor("A1", [128, 2048], F32, kind="ExternalInput").ap()
    A2_d = nc.dram_tensor("A2", [128, 1024], F32, kind="ExternalInput").ap()
    A3_d = nc.dram_tensor("A3", [128, 1024], F32, kind="ExternalInput").ap()
    A4_d = nc.dram_tensor("A4", [128, 8192], F32, kind="ExternalInput").ap()
    wpph_d = nc.dram_tensor("wpph", [128, 4096], F32, kind="ExternalInput").ap()
    poolw_d = nc.dram_tensor("poolw", [128, 512], F32, kind="ExternalInput").ap()
    cols_d = nc.dram_tensor("cols", [128, NCOL], F32, kind="ExternalInput").ap()
    ident_d = nc.dram_tensor("ident", [128, 128], F32, kind="ExternalInput").ap()
    tri_d = nc.dram_tensor("tri", [128, 128], F32, kind="ExternalInput").ap()
    invcnt_d = nc.dram_tensor("invcnt", [128, 16], F32, kind="ExternalInput").ap()
    gfin_d = nc.dram_tensor("gfin", [1, 1024], F32, kind="ExternalInput").ap()
    out_d = nc.dram_tensor("out", [NTOK, D], F32, kind="ExternalOutput").ap()
    wsc_d = nc.dram_tensor("wsc", [NPIECE, 128, 2048], BF16).ap()

    S = Sched(nc)
    U_UT, U_SAD, U_SBD, U_SAP, U_SBP = 0, 8448, 10560, 12672, 14784
    U_POOLED, U_CQ, U_CKV, U_KR, U_KRS, U_SQ, U_RSTD = 16896, 20992, 25088, 27136, 29184, 31232, 34304
    U_POSI, U_POSF, U_TS, U_TC = 38400, 40448, 42496, 44544
    U_SIZE = 47104
    U_FT, U_R = 0, 40448
    U_RT1, U_RT2 = U_SAD, U_SAD + 2048
    U_KT1, U_KT2 = U_SAP, U_SAP + 2048 + 64
    U_OT, U_YT, U_HS = 16896, 25088, 42496
    U_A4, U_STG, U_A1, U_A2, U_A3 = 0, 16384, 32768, 36864, 38912
    O_HT = U_SIZE
    O_MIX = O_HT + 8192
    O_CQN = O_MIX + 4096
    O_QABS = O_CQN + 2048
    O_QR = O_QABS + 8192
    O_PT = O_QR + 4096
    NPT = 6
    O_JUNK = O_PT + 1024 * NPT
    O_GT = O_JUNK + 2048
    A_SIZE = O_GT + 16384
    O_T1 = O_QABS
    U_ES, U_RS = 0, 4096
    U_ESP = 9216

    ctxs = []

    def sb(name, shape, dt):
        cm = nc.sbuf_tensor(name, shape, dt)
        t = cm.__enter__()
        ctxs.append(cm)
        return t

    def ps(name, shape, dt):
        cm = nc.psum_tensor(name, shape, dt)
        t = cm.__enter__()
        ctxs.append(cm)
        return t

    arena = sb("arena", [128, A_SIZE], U8)
    ring = sb("ring", [128, NSLOT * 2048], BF16)
    xb = sb("xb", [128, 2 * NT * D], F32)
    ckvnT = sb("ckvnT", [128, S_LEN], BF16)
    krlo = sb("krlo", [128, S_LEN], BF16)
    krhi = sb("krhi", [128, S_LEN], BF16)
    ckvtok = sb("ckvtok", [128, S_LEN], BF16)
    poolw = sb("poolw_s", [128, 512], BF16)
    wqr = sb("wqr_s", [128, 2048], BF16)
    wpps = sb("wpp_s", [128, 4096], BF16)
    wqa = sb("wqa_s", [128, 2048], BF16)
    identb = sb("identb", [128, 128], BF16)
    onesb = sb("onesb", [128, 128], BF16)
    onesf = sb("onesf", [128, 128], F32)
    trib = sb("trib", [128, 128], BF16)
    maskb = sb("maskb", [128, 128], BF16)
    cols = sb("cols_s", [128, NCOL], F32)
    bgh = sb("bgh", [128, 16], F32)
    invcnt = sb("invcnt_s", [128, 16], F32)
    gfin = sb("gfin_s", [128, 1024], F32)
    ssb = sb("ssb", [128, 48], F32)
    halo = sb("halo", [128, 64], F32)
    banks = [ps("bank%d" % i, [128, 512], F32) for i in range(8)]

    def AV(off, nbytes, dt):
        return arena[:, off:off + nbytes].bitcast(dt)

    def BK(b):
        return ("bank", b)

    uT = AV(U_UT, 8448, F32).rearrange("p (g t) -> p g t", g=4)
    sA = {"dve": AV(U_SAD, 2112, F32), "pool": AV(U_SAP, 2112, F32)}
    sB = {"dve": AV(U_SBD, 2112, F32), "pool": AV(U_SBP, 2112, F32)}
    sAk = {"dve": pg(U_SAD, 2112), "pool": pg(U_SAP, 2112)}
    sBk = {"dve": pg(U_SBD, 2112), "pool": pg(U_SBP, 2112)}
    pooledT = AV(U_POOLED, 4096, BF16).rearrange("p (g t) -> p g t", g=4)
    cq = AV(U_CQ, 4096, F32).rearrange("p (g t) -> p g t", g=2)
    ckv = AV(U_CKV, 2048, F32)
    kr = AV(U_KR, 2048, F32)
    krs = AV(U_KRS, 2048, F32)
    sq = AV(U_SQ, 3072, BF16).rearrange("p (g t) -> p g t", g=3)
    rstd = AV(U_RSTD, 4096, F32).rearrange("p (g t) -> p g t", g=2)
    posi = AV(U_POSI, 2048, I32)
    posf = AV(U_POSF, 2048, F32)
    ts_ = AV(U_TS, 2048, F32)
    tc_ = AV(U_TC, 2048, F32)
    fT = AV(U_FT, 32768, BF16).rearrange("p (g t) -> p g t", g=32)
    rbuf = [AV(U_R + 2048 * i, 2048, F32) for i in range(3)]
    rt1, rt2 = AV(U_RT1, 2048, F32), AV(U_RT2, 2048, F32)
    kt1, kt2 = AV(U_KT1, 2048, F32), AV(U_KT2, 2048, F32)
    oT = AV(U_OT, 8192, BF16).rearrange("p (g t) -> p g t", g=8)
    yT = AV(U_YT, 8192, BF16).rearrange("p (g t) -> p g t", g=8)
    HS_OFF = [43008, 45056, 40960, 34816]
    hs = [AV(o, 2048, BF16) for o in HS_OFF]
    hT = AV(O_HT, 8192, BF16).rearrange("p (g t) -> p g t", g=8)
    mixedT = AV(O_MIX, 4096, BF16).rearrange("p (g t) -> p g t", g=4)
    cqnT = AV(O_CQN, 2048, BF16).rearrange("p (g t) -> p g t", g=2)
    qabs = AV(O_QABS, 8192, BF16).rearrange("p (g t) -> p g t", g=8)
    qr = AV(O_QR, 4096, BF16).rearrange("p (g t) -> p g t", g=4)
    PT = [AV(O_PT + 1024 * i, 1024, BF16) for i in range(NPT)]
    rs = [AV(U_RS + 2048 * i, 2048, F32) for i in range(2)]
    esum = [AV(U_ES + 2048 * i, 2048, F32) for i in range(2)]
    esump = [AV(U_ESP + 2048 * i, 2048, F32) for i in range(2)]
    gts = AV(O_GT, 16384, BF16).rearrange("p (g t) -> p g t", g=16)
    junk = AV(O_JUNK, 2048, BF16)
    t12 = [[AV(O_T1 + 4096 * s + 2048 * i, 2048, F32) for i in range(2)] for s in range(2)]
    A4s = AV(U_A4, 16384, BF16).rearrange("p (h n) -> p h n", h=8)
    stg = AV(U_STG, 16384, BF16).rearrange("p (q j h m) -> p q j h m", q=4, j=2, h=8)
    A1s = AV(U_A1, 4096, BF16).rearrange("p (h r) -> p h r", h=8)
    A2s = AV(U_A2, 2048, BF16).rearrange("p (h c) -> p h c", h=8)
    A3s = AV(U_A3, 2048, BF16).rearrange("p (h c) -> p h c", h=8)
    xbv = xb[:].rearrange("p (b t d) -> p b t d", b=2, t=NT)
    ckvtokv = ckvtok[:].rearrange("p (b c) -> p b c", c=128)
    wqrv = wqr[:].rearrange("p (j k m) -> p j k m", j=8, k=2)
    wqav = wqa[:].rearrange("p (k h c) -> p k h c", k=2, h=8)
    poolwv = poolw[:].rearrange("p (g d) -> p g d", g=4)

    def col(i):
        return cols[:, i:i + 1]

    xsem = [S.new_sem("x%d" % b) for b in range(2)]
    osem = [S.new_sem("o%d" % b) for b in range(2)]
    psem = S.new_sem("pos")

    def load_x(ci):
        b = ci % 2
        S.dma("sp", xsem[b], xbv[:, b], x_d[ci * T:(ci + 1) * T, :].rearrange("(t p) d -> p t d", p=128),
              writes=[("x", b, tt) for tt in range(NT)])

    def load_pos(ci):
        S.dma("sp", psem, posi, pos_d[0:1, ci * T:(ci + 1) * T].partition_broadcast(128), writes=pg(U_POSI, 2048))

    load_x(0)
    if NCHUNK > 1:
        load_x(1)
    isem = [S.new_sem("i%d" % i) for i in range(16)]
    S.dma("sp", isem[0], cols[:], cols_d[:, :], writes=["cols"])
    S.dma("sp", isem[1], invcnt[:], invcnt_d[:, :], writes=["invcnt"])
    S.dma("sp", isem[2], gfin[:], gfin_d[0:1, :].partition_broadcast(128), writes=["gfin"])
    S.dma("pool", isem[3], identb[:], ident_d[:, :], writes=["ident"])
    S.dma("pool", isem[4], trib[:], tri_d[:, :], writes=["tri"])
    S.dma("pool", isem[5], poolw[:], poolw_d[:, :], writes=["poolw"])
    S.dma("pool", isem[6], wqr[:], wqr_d[:, :], writes=["wqr"])
    for q in range(2):
        S.dma("pool", isem[14 + q], wpps[:, 2048 * q:2048 * (q + 1)], wpph_d[:, 2048 * q:2048 * (q + 1)],
              writes=[("wpp", q)])
    S.dma("pool", isem[7], AV(U_A1, 4096, BF16), A1_d[:, :], writes=pg(U_A1, 4096))
    S.dma("pool", isem[8], AV(U_A2, 2048, BF16), A2_d[:, :], writes=pg(U_A2, 2048))
    S.dma("pool", isem[9], AV(U_A3, 2048, BF16), A3_d[:, :], writes=pg(U_A3, 2048))
    for i in range(4):
        S.dma("pool", isem[10 + i], AV(U_A4 + 4096 * i, 4096, BF16), A4_d[:, 2048 * i:2048 * (i + 1)],
              writes=pg(U_A4 + 4096 * i, 4096))
    for n, i in enumerate(HOST_PIECES):
        sname = S.new_sem("c%d" % i)
        S.dma("pool", sname, wsc_d[i], wst_d[n], reads=([("wsc", 12)] if n > 12 else []), writes=[("wsc", i)])
    S.op("dve", lambda e: e.memset(onesb[:], 1.0), writes=["ones"])
    S.op("dve", lambda e: e.tensor_scalar(out=maskb[:], in0=trib[:], scalar1=-1.0, scalar2=30000.0, op0=ALU.add,
                                           op1=ALU.mult), reads=["tri"], writes=["maskb"])
    S.op("dve", lambda e: e.memset(onesf[:], 1.0), writes=["onesf"])
    S.op("dve", lambda e: e.memset(krlo[:], 0.0), writes=[("krlo", c) for c in range(nch)])
    S.op("dve", lambda e: e.memset(krhi[:], 0.0), writes=[("krhi", c) for c in range(nch)])
    S.op("dve", lambda e: e.tensor_scalar(out=bgh[:], in0=cols[:, C_BG:C_BG + 16], scalar1=0.5, scalar2=None,
                                           op0=ALU.mult), reads=["cols"], writes=["bgh"])
    bctr = [0]

    def nb():
        b = bctr[0] % 8
        bctr[0] += 1
        return b

    for rc in range(2):
        for hg in range(2):
            b = nb()
            S.group("pe", [(lambda e, i=i, b=b, rc=rc, hg=hg: e.matmul(
                out=banks[b][:, i * 128:(i + 1) * 128], lhsT=A1s[:, hg * 4 + i, rc * 128:(rc + 1) * 128],
                rhs=A2s[:, hg * 4 + i, :], start=True, stop=True)) for i in range(4)],
                reads=[pg(U_A1, 4096), pg(U_A2, 2048)], writes=[BK(b)])
            S.op("dve", lambda e, b=b, rc=rc, hg=hg: e.tensor_copy(
                out=wqav[:, rc, hg * 4:(hg + 1) * 4, :],
                in_=banks[b][:].rearrange("p (h c) -> p h c", h=4)), writes=[BK(b), "wqa"])
    for h in range(8):
        for nh in range(2):
            b = nb()
            S.group("pe", [lambda e, b=b, h=h, nh=nh: e.matmul(out=banks[b][:], lhsT=A3s[:, h, :],
                                                                rhs=A4s[:, h, nh * 512:(nh + 1) * 512],
                                                                start=True, stop=True)],
                    reads=[pg(U_A3, 2048), pg(U_A4, 16384)], writes=[BK(b)])
            eng = "dve" if (h + nh) % 2 == 0 else "act"
            if eng == "dve":
                S.op("dve", lambda e, b=b, h=h, nh=nh: e.tensor_copy(
                    out=stg[:, 2 * nh:2 * nh + 2, :, h, :],
                    in_=banks[b][:].rearrange("p (q j m) -> p q j m", q=2, j=2)),
                    writes=[BK(b), pg(U_STG + 8192 * nh, 8192)])
            else:
                S.op("act", lambda e, b=b, h=h, nh=nh: e.activation(
                    out=stg[:, 2 * nh:2 * nh + 2, :, h, :],
                    in_=banks[b][:].rearrange("p (q j m) -> p q j m", q=2, j=2), func=AF.Copy),
                    writes=[BK(b), pg(U_STG + 8192 * nh, 8192)])
    for q in range(4):
        sname = S.new_sem("v%d" % q)
        S.dma("sp", sname, wsc_d[WVO_IDX[q]], AV(U_STG + 4096 * q, 4096, BF16),
              reads=pg(U_STG + 4096 * q, 4096), writes=[("wsc", WVO_IDX[q])])

    rsem = [S.new_sem("r%d" % s) for s in range(NSLOT)]
    rst = {"issued": 0, "acq": 0, "rel": 0}
    TOTAL = NCHUNK * NPIECE

    def ring_prefetch():
        while rst["issued"] < min(TOTAL, rst["rel"] + NSLOT):
            g = rst["issued"]
            s = g % NSLOT
            p = g % NPIECE
            S.dma("sp", rsem[s], ring[:, s * 2048:(s + 1) * 2048], wsc_d[p], reads=[("wsc", p)], writes=[("ring", s)])
            rst["issued"] += 1

    def ring_acquire(kind):
        g = rst["acq"]
        assert PIECES[g % NPIECE][0] == kind, (PIECES[g % NPIECE], kind)
        ring_prefetch()
        assert rst["issued"] > g
        rst["acq"] += 1
        s = g % NSLOT
        return s, ring[:, s * 2048:(s + 1) * 2048]

    def ring_release():
        rst["rel"] += 1
        ring_prefetch()

    def rmsnorm_to_hT(ci, gcol, tb):
        b = ci % 2
        for tt in range(NT):
            X = xbv[:, b, tt, :]
            sc = ssb[:, 32 + 4 * tt:32 + 4 * tt + 4]
            sk = ("ss", tt)
            S.op("act", lambda e, X=X, sc=sc: e.activation(out=junk, in_=X, func=AF.Square, accum_out=sc[:, 0:1]),
                 reads=[("x", b, tt)], writes=[pg(O_JUNK, 2048), sk])
            S.op("act", lambda e, sc=sc: e.activation(out=sc[:, 1:2], in_=sc[:, 0:1], func=AF.Ln, scale=1.0 / D,
                                                      bias=col(C_EPS)), reads=["cols"], writes=[sk])
            S.op("act", lambda e, sc=sc: e.activation(out=sc[:, 2:3], in_=sc[:, 1:2], func=AF.Exp, scale=-0.5),
                 writes=[sk])
            h_ = hs[tt]
            hk = pg(HS_OFF[tt], 2048)
            S.op("dve", lambda e, X=X, sc=sc, h_=h_: e.tensor_scalar(out=h_, in0=X, scalar1=sc[:, 2:3], scalar2=None,
                                                                      op0=ALU.mult),
                 reads=[("x", b, tt), sk], writes=[hk])
            S.group("pe", [(lambda e, kc=kc, h_=h_, tt=tt: e.transpose(
                out=banks[tb[kc // 2]][:].bitcast(BF16)[:, (kc % 2) * 512 + tt * 128:(kc % 2) * 512 + (tt + 1) * 128],
                in_=h_[:, kc * 128:(kc + 1) * 128], identity=identb[:])) for kc in range(8)],
                reads=[hk, "ident"], writes=[BK(tb[i]) for i in range(4)])
        for kc in range(8):
            src = banks[tb[kc // 2]][:].bitcast(BF16)[:, (kc % 2) * 512:(kc % 2) * 512 + 512]
            if (kc // 2) % 2 == 0:
                S.op("act", lambda e, kc=kc, src=src: e.activation(out=hT[:, kc, :], in_=src, func=AF.Copy,
                                                                   scale=col(gcol + kc)),
                     reads=["cols"], writes=[BK(tb[kc // 2]), pg(O_HT + 1024 * kc, 1024)])
            else:
                S.op("dve", lambda e, kc=kc, src=src: e.tensor_scalar(out=hT[:, kc, :], in0=src, scalar1=col(gcol + kc),
                                                                      scalar2=None, op0=ALU.mult),
                     reads=["cols"], writes=[BK(tb[kc // 2]), pg(O_HT + 1024 * kc, 1024)])

    U_HS2 = 0
    hs2 = [AV(U_HS2 + 2048 * i, 2048, BF16) for i in range(4)]

    def n2_stats(ci, tt):
        b = ci % 2
        X = xbv[:, b, tt, :]
        sc = ssb[:, 16 + 4 * tt:16 + 4 * tt + 4]
        sk = ("ss2", tt)
        S.op("act", lambda e, X=X, sc=sc: e.activation(out=junk, in_=X, func=AF.Square, accum_out=sc[:, 0:1]),
             reads=[("x", b, tt)], writes=[pg(O_JUNK, 2048), sk])
        S.op("act", lambda e, sc=sc: e.activation(out=sc[:, 1:2], in_=sc[:, 0:1], func=AF.Ln, scale=1.0 / D,
                                                  bias=col(C_EPS)), reads=["cols"], writes=[sk])
        S.op("act", lambda e, sc=sc: e.activation(out=sc[:, 2:3], in_=sc[:, 1:2], func=AF.Exp, scale=-0.5),
             writes=[sk])

    def n2_scale(ci, tt):
        b = ci % 2
        X = xbv[:, b, tt, :]
        sc = ssb[:, 16 + 4 * tt:16 + 4 * tt + 4]
        S.op("dve", lambda e, X=X, sc=sc, h_=hs2[tt]: e.tensor_scalar(out=h_, in0=X, scalar1=sc[:, 2:3], scalar2=None,
                                                                      op0=ALU.mult),
             reads=[("x", b, tt), ("ss2", tt)], writes=[pg(U_HS2 + 2048 * tt, 2048)])

    def n2_transposes(tt):
        h_ = hs2[tt]
        S.group("pe", [(lambda e, kc=kc, h_=h_, tt=tt: e.transpose(
            out=banks[kc // 2][:].bitcast(BF16)[:, (kc % 2) * 512 + tt * 128:(kc % 2) * 512 + (tt + 1) * 128],
            in_=h_[:, kc * 128:(kc + 1) * 128], identity=identb[:])) for kc in range(8)],
            reads=[pg(U_HS2 + 2048 * tt, 2048), "ident"], writes=[BK(i) for i in range(4)])

    def n2_evacs(gcol):
        for kc in range(8):
            src = banks[kc // 2][:].bitcast(BF16)[:, (kc % 2) * 512:(kc % 2) * 512 + 512]
            if kc in (0, 3, 6):
                S.op("act", lambda e, kc=kc, src=src: e.activation(out=hT[:, kc, :], in_=src, func=AF.Copy,
                                                                   scale=col(gcol + kc)),
                     reads=["cols"], writes=[BK(kc // 2), pg(O_HT + 1024 * kc, 1024)])
            else:
                S.op("dve", lambda e, kc=kc, src=src: e.tensor_scalar(out=hT[:, kc, :], in0=src, scalar1=col(gcol + kc),
                                                                      scalar2=None, op0=ALU.mult),
                     reads=["cols"], writes=[BK(kc // 2), pg(O_HT + 1024 * kc, 1024)])

    HTK = pg(O_HT, 8192)
    evt = [0]

    def evac_copy(dst, dkeys, b, scale_ap=None, extra_reads=()):
        evt[0] += 1
        if evt[0] % 2 == 0:
            if scale_ap is None:
                S.op("act", lambda e: e.activation(out=dst, in_=banks[b][:], func=AF.Copy), reads=list(extra_reads),
                     writes=[BK(b), dkeys])
            else:
                S.op("act", lambda e: e.activation(out=dst, in_=banks[b][:], func=AF.Copy, scale=scale_ap),
                     reads=list(extra_reads), writes=[BK(b), dkeys])
        else:
            if scale_ap is None:
                S.op("dve", lambda e: e.tensor_copy(out=dst, in_=banks[b][:]), reads=list(extra_reads),
                     writes=[BK(b), dkeys])
            else:
                S.op("dve", lambda e: e.tensor_scalar(out=dst, in0=banks[b][:], scalar1=scale_ap, scalar2=None,
                                                      op0=ALU.mult), reads=list(extra_reads), writes=[BK(b), dkeys])

    zdst = [(uT[:, g, 16:528], pg(U_UT + 2112 * g, 2112)) for g in range(4)] + \
           [(cq[:, 0, :], pg(U_CQ, 2048)), (cq[:, 1, :], pg(U_CQ + 2048, 2048)),
            (ckv, pg(U_CKV, 2048)), (kr, pg(U_KR, 2048)), (krs, pg(U_KRS, 2048))]

    def z_piece(zp):
        s, rp = ring_acquire("z")
        rv = rp.rearrange("p (j k m) -> p j k m", j=2, k=8)
        for j, zi in enumerate(ZSETS[zp]):
            b = nb()
            S.group("pe", [(lambda e, kc=kc, j=j, rv=rv, b=b: e.matmul(out=banks[b][:], lhsT=rv[:, j, kc, :],
                                                                       rhs=hT[:, kc, :], start=(kc == 0),
                                                                       stop=(kc == 7))) for kc in range(8)],
                    reads=[("ring", s), HTK], writes=[BK(b)])
            evac_copy(zdst[zi][0], zdst[zi][1], b)
        ring_release()

    def pool_group(c, g):
        e_ = "dve" if g < 2 else "pool"
        w = 2 ** (g + 1)
        uk = pg(U_UT + 2112 * g, 2112)
        src, srck = uT[:, g, :], uk
        bufs = [(sA[e_], sAk[e_]), (sB[e_], sBk[e_])]
        for k in range(g + 1):
            sh = 2 ** k
            lo = 2 ** (k + 1) - 1
            dst, dstk = bufs[k % 2]
            S.op(e_, lambda e, src=src, dst=dst, sh=sh, lo=lo: e.tensor_tensor(
                out=dst[:, lo:528], in0=src[:, lo:528], in1=src[:, lo - sh:528 - sh], op=ALU.add),
                reads=[srck], writes=[dstk])
            src, srck = dst, dstk
        S.op("dve", lambda e, src=src, g=g, w=w: e.scalar_tensor_tensor(
            out=pooledT[:, g, :], in0=src[:, 16:528], scalar=1.0 / w, in1=uT[:, g, 16:528],
            op0=ALU.mult, op1=ALU.subtract), reads=[srck, uk], writes=pg(U_POOLED + 1024 * g, 1024))
        if c == 0:
            other = bufs[(g + 1) % 2]
            S.op(e_, lambda e, src=src, other=other, w=w: e.tensor_tensor(
                out=other[0][:, 0:w - 1], in0=src[:, 16:16 + w - 1], in1=invcnt[:, 0:w - 1], op=ALU.mult),
                reads=[srck, "invcnt"], writes=[other[1]])
            S.op(e_, lambda e, other=other, g=g, w=w: e.tensor_tensor(
                out=pooledT[:, g, 0:w - 1], in0=other[0][:, 0:w - 1], in1=uT[:, g, 16:16 + w - 1],
                op=ALU.subtract), reads=[other[1], uk], writes=pg(U_POOLED + 1024 * g, 1024))
        S.op(e_, lambda e, g=g: e.tensor_copy(out=halo[:, 16 * g:16 * (g + 1)], in_=uT[:, g, 512:528]),
             reads=[uk], writes=[("halo", g)])

    def pool_linear(g):
        b = nb()
        S.group("pe", [lambda e, g=g, b=b: e.matmul(out=banks[b][:], lhsT=poolwv[:, g, :], rhs=pooledT[:, g, :],
                                                    start=True, stop=True)],
                reads=["poolw", pg(U_POOLED + 1024 * g, 1024)], writes=[BK(b)])
        evac_copy(mixedT[:, g, :], pg(O_MIX + 1024 * g, 1024), b, scale_ap=col(C_PSC + g), extra_reads=["cols"])

    def gate_pre(mc):
        s, rp = ring_acquire("g")
        gv = rp.rearrange("p (j k m) -> p j k m", j=2, k=8)
        for i in range(2):
            b = nb()
            S.group("pe", [(lambda e, kc=kc, i=i, b=b: e.matmul(out=banks[b][:], lhsT=gv[:, i, kc, :],
                                                                rhs=hT[:, kc, :], start=(kc == 0), stop=(kc == 7)))
                           for kc in range(8)], reads=[("ring", s), HTK], writes=[BK(b)])
            S.op("act", lambda e, i=i, b=b: e.activation(out=gts[:, 8 * i + mc, :], in_=banks[b][:], func=AF.Tanh,
                                                         scale=0.5, bias=bgh[:, 8 * i + mc:8 * i + mc + 1]),
                 reads=["bgh"], writes=[BK(b), pg(O_GT + 1024 * (8 * i + mc), 1024)])
        ring_release()

    SINK, COSK = pg(U_TS, 2048), pg(U_TC, 2048)
    CQNK = pg(O_CQN, 2048)

    def stream_A(ci):
        c = ci % nch
        S.op("act", lambda e: e.activation(out=sq[:, 0:2, :], in_=cq[:, :, :], func=AF.Square),
             reads=pg(U_CQ, 4096), writes=pg(U_SQ, 2048))
        S.op("act", lambda e: e.activation(out=sq[:, 2, :], in_=ckv, func=AF.Square),
             reads=pg(U_CKV, 2048), writes=pg(U_SQ + 2048, 1024))
        S.op("dve", lambda e: e.tensor_copy(out=posf, in_=posi), reads=pg(U_POSI, 2048), writes=pg(U_POSF, 2048))
        S.op("pool", lambda e: e.tensor_scalar(out=ts_, in0=posf, scalar1=col(C_INVF), scalar2=0.0, op0=ALU.mult,
                                               op1=ALU.add),
             reads=[pg(U_POSF, 2048), "cols"], writes=pg(U_TS, 2048))
        S.op("pool", lambda e: e.tensor_scalar(out=tc_, in0=ts_, scalar1=1.0, scalar2=0.25, op0=ALU.mult,
                                               op1=ALU.add),
             reads=pg(U_TS, 2048), writes=pg(U_TC, 2048))
        yield
        bq, bkv = nb(), nb()
        S.group("pe", [(lambda e, i=i: e.matmul(out=banks[bq][:], lhsT=onesb[:], rhs=sq[:, i, :], start=(i == 0),
                                                stop=(i == 1))) for i in range(2)],
                reads=["ones", pg(U_SQ, 2048)], writes=[BK(bq)])
        S.group("pe", [lambda e: e.matmul(out=banks[bkv][:], lhsT=onesb[:], rhs=sq[:, 2, :], start=True, stop=True)],
                reads=["ones", pg(U_SQ + 2048, 1024)], writes=[BK(bkv)])
        for i, (bb, n) in enumerate(((bq, 256), (bkv, 128))):
            rk = pg(U_RSTD + 2048 * i, 2048)
            S.op("act", lambda e, i=i, bb=bb, n=n: e.activation(out=rstd[:, i, :], in_=banks[bb][:], func=AF.Ln,
                                                                scale=1.0 / n, bias=col(C_EPS)),
                 reads=["cols"], writes=[BK(bb), rk])
            S.op("act", lambda e, i=i: e.activation(out=rstd[:, i, :], in_=rstd[:, i, :], func=AF.Exp, scale=-0.5),
                 writes=[rk])
        for (tv, tk) in ((ts_, pg(U_TS, 2048)), (tc_, pg(U_TC, 2048))):
            S.op("dve", lambda e, tv=tv: e.tensor_copy(out=posi, in_=tv), reads=[tk], writes=pg(U_POSI, 2048))
            S.op("dve", lambda e: e.tensor_copy(out=posf, in_=posi), reads=pg(U_POSI, 2048), writes=pg(U_POSF, 2048))
            S.op("pool", lambda e, tv=tv: e.tensor_tensor(out=tv, in0=tv, in1=posf, op=ALU.subtract),
                 reads=pg(U_POSF, 2048), writes=[tk])
        yield
        for rc in range(2):
            S.op("dve", lambda e, rc=rc: e.scalar_tensor_tensor(out=cqnT[:, rc, :], in0=cq[:, rc, :],
                                                                scalar=col(C_GQ + rc), in1=rstd[:, 0, :],
                                                                op0=ALU.mult, op1=ALU.mult),
                 reads=[pg(U_CQ + 2048 * rc, 2048), pg(U_RSTD, 2048), "cols"], writes=pg(O_CQN + 1024 * rc, 1024))
        S.op("dve", lambda e: e.scalar_tensor_tensor(out=ckvnT[:, c * T:(c + 1) * T], in0=ckv, scalar=col(C_GKV),
                                                     in1=rstd[:, 1, :], op0=ALU.mult, op1=ALU.mult),
             reads=[pg(U_CKV, 2048), pg(U_RSTD + 2048, 2048), "cols"], writes=[("ckvnT", c)])
        S.op("act", lambda e: e.activation(out=ts_, in_=ts_, func=AF.Sin, scale=col(C_SGN)), reads=["cols"],
             writes=pg(U_TS, 2048))
        S.op("act", lambda e: e.activation(out=tc_, in_=tc_, func=AF.Sin, scale=float(2 * np.pi * (1 - 1e-6))),
             writes=pg(U_TC, 2048))
        yield
        bt = nb()
        S.group("pe", [(lambda e, i=i: e.transpose(out=banks[bt][:].bitcast(BF16)[:, i * 128:(i + 1) * 128],
                                                   in_=ckvnT[:, c * T + i * 128:c * T + (i + 1) * 128],
                                                   identity=identb[:])) for i in range(4)],
                reads=[("ckvnT", c), "ident"], writes=[BK(bt)])
        S.op("dve", lambda e: e.tensor_copy(out=ckvtok[:, c * T:(c + 1) * T], in_=banks[bt][:].bitcast(BF16)[:, 0:512]),
             writes=[BK(bt), ("ckvtok", c)])
        S.op("dve", lambda e: e.tensor_tensor(out=kt1, in0=kr, in1=tc_, op=ALU.mult), reads=[pg(U_KR, 2048), COSK],
             writes=pg(U_KT1, 2048))
        S.op("pool", lambda e: e.tensor_tensor(out=kt2, in0=krs, in1=ts_, op=ALU.mult), reads=[pg(U_KRS, 2048), SINK],
             writes=pg(U_KT2, 2048))
        S.op("dve", lambda e: e.tensor_tensor(out=krlo[0:64, c * T:(c + 1) * T], in0=kt1[0:64, :], in1=kt2[0:64, :],
                                              op=ALU.add), reads=[pg(U_KT1, 2048), pg(U_KT2, 2048)],
             writes=[("krlo", c)])
        S.op("dve", lambda e: e.tensor_tensor(out=krhi[64:128, c * T:(c + 1) * T], in0=kt1[64:128, :],
                                              in1=kt2[64:128, :], op=ALU.add),
             reads=[pg(U_KT1, 2048), pg(U_KT2, 2048)], writes=[("krhi", c)])
        for h in range(NH):
            b = nb()
            S.group("pe", [(lambda e, rc=rc, h=h, b=b: e.matmul(out=banks[b][:], lhsT=wqav[:, rc, h, :],
                                                                rhs=cqnT[:, rc, :], start=(rc == 0), stop=(rc == 1)))
                           for rc in range(2)], reads=["wqa", CQNK], writes=[BK(b)])
            evac_copy(qabs[:, h, :], pg(O_QABS + 1024 * h, 1024), b)
        yield
        for j in range(4):
            b1, b2 = nb(), nb()
            S.group("pe", [(lambda e, rc=rc, j=j: e.matmul(out=banks[b1][:], lhsT=wqrv[:, j, rc, :], rhs=cqnT[:, rc, :],
                                                           start=(rc == 0), stop=(rc == 1))) for rc in range(2)],
                    reads=["wqr", CQNK], writes=[BK(b1)])
            S.group("pe", [(lambda e, rc=rc, j=j: e.matmul(out=banks[b2][:], lhsT=wqrv[:, 4 + j, rc, :],
                                                           rhs=cqnT[:, rc, :], start=(rc == 0), stop=(rc == 1)))
                           for rc in range(2)], reads=["wqr", CQNK], writes=[BK(b2)])
            S.op("dve", lambda e, b1=b1: e.tensor_tensor(out=rt1, in0=banks[b1][:], in1=tc_, op=ALU.mult),
                 reads=[COSK], writes=[BK(b1), pg(U_RT1, 2048)])
            S.op("dve", lambda e, b2=b2: e.tensor_tensor(out=rt2, in0=banks[b2][:], in1=ts_, op=ALU.mult),
                 reads=[SINK], writes=[BK(b2), pg(U_RT2, 2048)])
            S.op("pool", lambda e, j=j: e.tensor_tensor(out=qr[:, j, :], in0=rt1, in1=rt2, op=ALU.add),
                 reads=[pg(U_RT1, 2048), pg(U_RT2, 2048)], writes=pg(O_QR + 1024 * j, 1024))
            if j % 2 == 1:
                yield

    def stream_B(ci):
        c = ci % nch
        z_piece(3)
        z_piece(4)
        pool_group(c, 2)
        pool_group(c, 3)
        yield
        gate_pre(0)
        gate_pre(1)
        yield
        gate_pre(2)
        gate_pre(3)
        pool_group(c, 0)
        pool_group(c, 1)
        yield
        gate_pre(4)
        gate_pre(5)
        yield
        gate_pre(6)
        gate_pre(7)
        yield
        for g in range(4):
            pool_linear(g)
        yield

    SBANKS = [0, 1, 2, 7]
    LOOK = 2

    def attention(ci):
        c = ci % nch
        nkb = 4 * c + 4
        n_o = 4 * c
        korder = []
        for i in range(4):
            korder.extend(range((i * n_o) // 4, ((i + 1) * n_o) // 4))
            korder.append(n_o + i)
        assert sorted(korder) == list(range(nkb)) and korder[0] == 0
        units = [(h, kb) for h in range(NH) for kb in korder]
        KFIRST, KLAST = korder[0], korder[-1]
        NU = len(units)

        def emit_S(u):
            h, kb = units[u]
            off = max(0, kb - 4 * c) * 128
            sbk = SBANKS[u % 4]
            krc = krlo if h % 2 == 0 else krhi
            krk = ("krlo" if h % 2 == 0 else "krhi", kb // 4)
            fns = [lambda e: e.matmul(out=banks[sbk][:, off:512], lhsT=ckvnT[:, kb * 128:(kb + 1) * 128],
                                      rhs=qabs[:, h, off:512], start=True, stop=False)]
            if kb >= 4 * c:
                fns.append(lambda e: e.matmul(out=banks[sbk][:, off:off + 128], lhsT=identb[:], rhs=maskb[:],
                                              start=False, stop=False))
            fns.append(lambda e: e.matmul(out=banks[sbk][:, off:512], lhsT=krc[:, kb * 128:(kb + 1) * 128],
                                          rhs=qr[:, h // 2, off:512], start=False, stop=True))
            S.group("pe", fns,
                    reads=[("ckvnT", kb // 4), krk, pg(O_QABS + 1024 * h, 1024), pg(O_QR + 1024 * (h // 2), 1024),
                           "ident", "maskb"],
                    writes=[BK(sbk)])

        D1, D2 = min(6, nkb - 3), min(8, nkb - 1)
        pend = []
        PESUM = 10
        pesum_n = [0] * NH

        def fin_pe(h):
            sb_ = 5 + (h % 2)
            es = esum[h % 2]
            esk = pg(U_ES + 2048 * (h % 2), 2048)
            S.phase = "ATTF"
            first = pesum_n[h] == 0
            S.group("pe", [lambda e: e.matmul(out=banks[sb_][:], lhsT=onesf[:], rhs=es, start=first, stop=True)],
                    reads=["onesf", esk], writes=[BK(sb_)])
            S.phase = "ATT"

        def fin_rest(h):
            ob, sb_ = 3 + (h % 2), 5 + (h % 2)
            r_ = rs[h % 2]
            rk = pg(U_RS + 2048 * (h % 2), 2048)
            S.op("act", lambda e: e.activation(out=r_, in_=banks[sb_][:], func=AF.Ln), writes=[BK(sb_), rk])
            S.op("act", lambda e: e.activation(out=r_, in_=r_, func=AF.Exp, scale=-1.0), writes=[rk])
            S.op("dve", lambda e: e.tensor_tensor(out=oT[:, h, :], in0=banks[ob][:], in1=r_, op=ALU.mult),
                 reads=[rk], writes=[BK(ob), pg(U_OT + 1024 * h, 1024)])

        for u in range(min(LOOK, NU)):
            emit_S(u)
        for u in range(NU):
            h, kb = units[u]
            off = max(0, kb - 4 * c) * 128
            if u + LOOK < NU:
                emit_S(u + LOOK)
            sbk = SBANKS[u % 4]
            pt = PT[u % NPT]
            ptk = pg(O_PT + 1024 * (u % NPT), 1024)
            S.op("act", lambda e, sbk=sbk, pt=pt, off=off: e.activation(out=pt[:, off:512], in_=banks[sbk][:, off:512],
                                                                        func=AF.Exp, scale=SCALE),
                 writes=[BK(sbk), ptk])
            ob, sb_ = 3 + (h % 2), 5 + (h % 2)
            es = esum[h % 2]
            esk = pg(U_ES + 2048 * (h % 2), 2048)
            S.group("pe", [
                lambda e, ob=ob, pt=pt, off=off, kb=kb: e.matmul(out=banks[ob][:, off:512], lhsT=ckvtokv[:, kb, :],
                                                                 rhs=pt[:, off:512], start=(kb == KFIRST),
                                                                 stop=(kb == KLAST))],
                reads=[("ckvtok", kb // 4), ptk], writes=[BK(ob)])
            if kb < 4 * c and kb % PESUM == PESUM - 1:
                first = pesum_n[h] == 0
                pesum_n[h] += 1
                S.group("pe", [lambda e, sb_=sb_, pt=pt, first=first: e.matmul(out=banks[sb_][:], lhsT=onesb[:], rhs=pt,
                                                                               start=first, stop=False)],
                        reads=["ones", ptk], writes=[BK(sb_)])
            elif kb == KFIRST:
                S.op("dve", lambda e, es=es, pt=pt: e.tensor_copy(out=es, in_=pt), reads=[ptk], writes=[esk])
            else:
                S.op("dve", lambda e, es=es, pt=pt, off=off: e.tensor_tensor(out=es[:, off:512], in0=es[:, off:512],
                                                                            in1=pt[:, off:512], op=ALU.add),
                     reads=[ptk], writes=[esk])
            for p in list(pend):
                if p[2] == 0 and u - p[1] >= D1:
                    fin_pe(p[0])
                    p[2] = 1
                elif p[2] == 1 and u - p[1] >= D2:
                    fin_rest(p[0])
                    pend.remove(p)
            if kb == KLAST:
                for p in list(pend):
                    if p[2] == 0:
                        fin_pe(p[0])
                    fin_rest(p[0])
                    pend.remove(p)
                pend.append([h, u, 0])
        for p in list(pend):
            if p[2] == 0:
                fin_pe(p[0])
            fin_rest(p[0])

    def gate_phase(ci):
        MIXK, OTK = pg(O_MIX, 4096), pg(U_OT, 8192)
        wvo_s = None
        AAB, BMB = [0, 2, 7, 1], [3, 4, 5, 6]

        def aa(mc):
            wppv = wpps[:, 2048 * (mc // 4):2048 * (mc // 4 + 1)].rearrange("p (j k m) -> p j k m", j=4, k=4)
            b = AAB[mc % 4]
            S.group("pe", [(lambda e, kc=kc: e.matmul(out=banks[b][:], lhsT=wppv[:, mc % 4, kc, :],
                                                      rhs=mixedT[:, kc, :], start=(kc == 0), stop=(kc == 3)))
                           for kc in range(4)], reads=[("wpp", mc // 4), MIXK], writes=[BK(b)])

        for mc in range(4):
            aa(mc)
        for mc in range(8):
            if mc % 2 == 0:
                wvo_s = ring_acquire("wvo")
            wvov = wvo_s[1].rearrange("p (j h m) -> p j h m", j=2, h=8)
            bs = [AAB[mc % 4], BMB[mc % 4]]
            S.group("pe", [(lambda e, h=h: e.matmul(out=banks[bs[1]][:], lhsT=wvov[:, mc % 2, h, :], rhs=oT[:, h, :],
                                                    start=(h == 0), stop=(h == 7))) for h in range(8)],
                    reads=[("ring", wvo_s[0]), OTK], writes=[BK(bs[1])])
            st = mc % 2
            for i in range(2):
                S.op("dve", lambda e, i=i: e.scalar_tensor_tensor(out=t12[st][i], in0=gts[:, 8 * i + mc, :], scalar=1.0,
                                                                  in1=banks[bs[i]][:], op0=ALU.add, op1=ALU.mult),
                     reads=pg(O_GT + 1024 * (8 * i + mc), 1024),
                     writes=[BK(bs[i]), pg(O_T1 + 4096 * st + 2048 * i, 2048)])
            S.op("pool", lambda e: e.tensor_tensor(out=yT[:, mc, :], in0=t12[st][0], in1=t12[st][1], op=ALU.add),
                 reads=pg(O_T1 + 4096 * st, 4096), writes=pg(U_YT + 1024 * mc, 1024))
            if mc % 2 == 1:
                ring_release()
            if mc + 4 < 8:
                aa(mc + 4)

    def emit_store(ci):
        b = ci % 2
        S.dma("sp", osem[b], out_d[ci * T:(ci + 1) * T, :].rearrange("(t p) d -> p t d", p=128), xbv[:, b],
              reads=[("x", b, tt) for tt in range(NT)])

    load_pos(0)
    S.phase = "N1"
    rmsnorm_to_hT(0, C_GMIX, [nb() for _ in range(4)])
    for ci in range(NCHUNK):
        c = ci % nch
        xbk = ci % 2
        XK = [("x", xbk, tt) for tt in range(NT)]
        S.phase = "Z"
        bctr[0] = 0
        if c == 0:
            S.op("pool", lambda e: e.memset(uT[:, :, 0:16], 0.0), writes=pg(U_UT, 8448))
        else:
            S.op("pool", lambda e: e.tensor_copy(out=uT[:, :, 0:16], in_=halo[:].rearrange("p (g t) -> p g t", g=4)),
                 reads=[("halo", g) for g in range(4)], writes=pg(U_UT, 8448))
        for zp in range(3):
            z_piece(zp)
        S.phase = "MIX"
        gens = [stream_A(ci), stream_B(ci)]
        alive = [True, True]
        while any(alive):
            for gi in range(2):
                if alive[gi]:
                    try:
                        next(gens[gi])
                    except StopIteration:
                        alive[gi] = False
        if ci >= 1:
            emit_store(ci - 1)
            if ci + 1 < NCHUNK:
                load_x(ci + 1)
        if ci + 1 < NCHUNK:
            load_pos(ci + 1)
        S.phase = "ATT"
        attention(ci)
        S.phase = "GATE"
        gate_phase(ci)
        S.phase = "WOUT"
        YTK = pg(U_YT, 8192)
        for nh in range(2):
            pcs = [ring_acquire("wo"), ring_acquire("wo")]
            def wo_mm(r, tt):
                pv = pcs[r][1].rearrange("p (k n) -> p k n", k=4)
                b = 4 * nh + tt
                S.group("pe", [(lambda e, k4=k4, pv=pv, b=b, tt=tt, r=r: e.matmul(
                    out=banks[b][:], lhsT=yT[:, 4 * r + k4, tt * 128:(tt + 1) * 128], rhs=pv[:, k4, :],
                    start=(r == 0 and k4 == 0), stop=(r == 1 and k4 == 3))) for k4 in range(4)],
                    reads=[("ring", pcs[r][0]), pg(U_YT + 4096 * r, 4096)], writes=[BK(b)])

            def wo_add(tt):
                b = 4 * nh + tt
                Xh = xbv[:, xbk, tt, nh * 512:(nh + 1) * 512]
                S.op("dve", lambda e, b=b, Xh=Xh: e.scalar_tensor_tensor(out=Xh, in0=banks[b][:], scalar=0.5, in1=Xh,
                                                                         op0=ALU.mult, op1=ALU.add),
                     writes=[BK(b), ("x", xbk, tt)])

            if nh == 0:
                for r in range(2):
                    for tt in range(NT):
                        wo_mm(r, tt)
                ring_release()
                ring_release()
                for tt in range(NT):
                    wo_add(tt)
            else:
                for tt in range(NT):
                    S.phase = "WOUT"
                    for r in range(2):
                        wo_mm(r, tt)
                    wo_add(tt)
                    S.phase = "N2"
                    n2_stats(ci, tt)
                    if tt >= 1:
                        n2_scale(ci, tt - 1)
                    if tt >= 2:
                        n2_transposes(tt - 2)
                ring_release()
                ring_release()
                n2_scale(ci, NT - 1)
                n2_transposes(NT - 2)
                n2_transposes(NT - 1)
                n2_evacs(C_GMLP)
        S.phase = "MLP1"
        bctr[0] = 4
        for q in range(16):
            s, rp = ring_acquire("w1")
            rv = rp.rearrange("p (j k m) -> p j k m", j=2, k=8)
            for j in range(2):
                fc = 2 * q + j
                b = nb()
                if fc == 0:
                    for kc in range(8):
                        S.group("pe", [lambda e, kc=kc, j=j, rv=rv, b=b: e.matmul(
                            out=banks[b][:], lhsT=rv[:, j, kc, :], rhs=hT[:, kc, :], start=(kc == 0), stop=(kc == 7))],
                            reads=[("ring", s), pg(O_HT + 1024 * kc, 1024)], writes=[BK(b)])
                else:
                    S.group("pe", [(lambda e, kc=kc, j=j, rv=rv, b=b: e.matmul(out=banks[b][:], lhsT=rv[:, j, kc, :],
                                                                               rhs=hT[:, kc, :], start=(kc == 0),
                                                                               stop=(kc == 7))) for kc in range(8)],
                            reads=[("ring", s), HTK], writes=[BK(b)])
                rb = rbuf[fc % 3]
                rk = pg(U_R + 2048 * (fc % 3), 2048)
                S.op("act", lambda e, b=b, rb=rb: e.activation(out=rb, in_=banks[b][:], func=AF.Relu),
                     writes=[BK(b), rk])
                e2 = "pool" if fc % 3 == 2 else "dve"
                S.op(e2, lambda e, rb=rb, fc=fc: e.tensor_tensor(out=fT[:, fc, :], in0=rb, in1=rb, op=ALU.mult),
                     reads=[rk], writes=pg(U_FT + 1024 * fc, 1024))
            ring_release()
        S.phase = "MLP2"
        for nh in range(2):
            for r in range(8):
                s, rp = ring_acquire("w2")
                pv = rp.rearrange("p (k n) -> p k n", k=4)
                for tt in range(NT):
                    b = 4 * nh + tt
                    S.group("pe", [(lambda e, k4=k4, pv=pv, b=b, tt=tt, r=r: e.matmul(
                        out=banks[b][:], lhsT=fT[:, 4 * r + k4, tt * 128:(tt + 1) * 128], rhs=pv[:, k4, :],
                        start=(r == 0 and k4 == 0), stop=(r == 7 and k4 == 3))) for k4 in range(4)],
                        reads=[("ring", s), pg(U_FT + 4096 * r, 4096)], writes=[BK(b)])
                ring_release()
                if nh == 1 and r == 3 and ci + 1 < NCHUNK:
                    S.phase = "N1"
                    rmsnorm_to_hT(ci + 1, C_GMIX, [0, 1, 2, 3])
                    S.phase = "MLP2"
            for tt in range(NT):
                b = 4 * nh + tt
                Xh = xbv[:, xbk, tt, nh * 512:(nh + 1) * 512]
                S.op("dve", lambda e, b=b, Xh=Xh: e.tensor_tensor(out=Xh, in0=banks[b][:], in1=Xh, op=ALU.add),
                     writes=[BK(b), ("x", xbk, tt)])
        S.phase = "FIN"
        for tt in range(NT):
            X = xbv[:, xbk, tt, :]
            sc = ssb[:, 8 + 4 * (tt % 2):8 + 4 * (tt % 2) + 4]
            sk = ("ssf", tt % 2)
            S.op("act", lambda e, X=X, sc=sc: e.activation(out=junk, in_=X, func=AF.Square, accum_out=sc[:, 0:1]),
                 reads=[("x", xbk, tt)], writes=[pg(O_JUNK, 2048), sk])
            S.op("act", lambda e, sc=sc: e.activation(out=sc[:, 1:2], in_=sc[:, 0:1], func=AF.Ln, scale=1.0 / D,
                                                      bias=col(C_EPS)), reads=["cols"], writes=[sk])
            S.op("act", lambda e, sc=sc: e.activation(out=sc[:, 2:3], in_=sc[:, 1:2], func=AF.Exp, scale=-0.5),
                 writes=[sk])
            S.op("dve", lambda e, X=X, sc=sc: e.scalar_tensor_tensor(out=X, in0=X, scalar=sc[:, 2:3], in1=gfin[:],
                                                                     op0=ALU.mult, op1=ALU.mult),
                 reads=[sk, "gfin"], writes=[("x", xbk, tt)])
        if ci == NCHUNK - 1:
            emit_store(ci)
    S._wait("sp", [(osem[b], S.dcnt.get(osem[b], 0)) for b in range(2) if S.dcnt.get(osem[b], 0) > 0])
    for cm in reversed(ctxs):
        cm.__exit__(None, None, None)
    S.close()
    nc._pe_labels = S.pe_labels
    return nc


_CACHE = {}


def kernel(**inputs):
    x = np.asarray(inputs["x"], np.float32)
    pos = np.asarray(inputs["positions"], np.int32)
    B, SL, _ = x.shape
    nseq = B // N_CORES
    nch = SL // T
    hp = host_prep(inputs)
    key = (nseq, nch)
    if key not in _CACHE:
        _CACHE[key] = build(nseq, nch)
    nc = _CACHE[key]
    in_maps = []
    for c in range(N_CORES):
        m = dict(hp)
        m["x"] = np.ascontiguousarray(x[c * nseq:(c + 1) * nseq].reshape(nseq * SL, D))
        m["pos"] = np.ascontiguousarray(pos[c * nseq:(c + 1) * nseq].reshape(1, nseq * SL))
        in_maps.append(m)
    res = run_bass_kernel_spmd(nc, in_maps, core_ids=list(range(N_CORES)))
    out = np.concatenate([np.asarray(r["out"], np.float32).reshape(nseq, SL, D) for r in res.results], axis=0)
    return out
```

```python
import math
import numpy as np
import concourse.bass as bass
import concourse.mybir as mybir
from concourse.bass_utils import run_bass_kernel_spmd

F32 = mybir.dt.float32
BF16 = mybir.dt.bfloat16
I32 = mybir.dt.int32
U8 = mybir.dt.uint8
ALU = mybir.AluOpType
AF = mybir.ActivationFunctionType

D = 1024
T = 512
NT = 4
NH = 8
DFF = 4096
EPS = 1e-6
SCALE = 1.0 / math.sqrt(192.0)
NSLOT = 5
PAGE = 1024
N_CORES = 8

C_GMIX, C_GMLP, C_PSC, C_GQ, C_GKV, C_BG, C_INVF, C_SGN, C_EPS, C_QUART, NCOL = 0, 8, 16, 20, 22, 23, 39, 40, 41, 42, 43


def _flat(keys):
    out = []
    for k in keys:
        if isinstance(k, list):
            out.extend(_flat(k))
        else:
            out.append(k)
    return out


class Sched:
    def __init__(self, nc):
        self.nc = nc
        self.eng = {"pe": nc.tensor, "dve": nc.vector, "act": nc.scalar, "pool": nc.gpsimd, "sp": nc.sync}
        self.semobj = {}
        self._ctx = []
        self.cnt = {}
        for k in self.eng:
            self.new_sem(k)
            self.cnt[k] = 0
        self.waited = {k: {} for k in self.eng}
        self.lastw = {}
        self.readers = {}
        self.dcnt = {}
        self.phase = "init"
        self.pe_labels = []

    def new_sem(self, name):
        cm = self.nc.semaphore("s_" + name)
        self.semobj[name] = cm.__enter__()
        self._ctx.append(cm)
        return name

    def close(self):
        for cm in reversed(self._ctx):
            cm.__exit__(None, None, None)

    def _deps(self, reads, writes):
        toks = []
        for k in reads:
            t = self.lastw.get(k)
            if t is not None:
                toks.append(t)
        for k in writes:
            t = self.lastw.get(k)
            if t is not None:
                toks.append(t)
            toks.extend(self.readers.get(k, ()))
        return toks

    def _pending(self, e, toks):
        need = {}
        for (s, v) in toks:
            if v > need.get(s, 0):
                need[s] = v
        w = self.waited[e]
        pend = []
        for s, v in need.items():
            if w.get(s, 0) < v:
                pend.append((s, v))
                w[s] = v
        return pend

    def _wait(self, e, toks):
        for s, v in self._pending(e, toks):
            self.eng[e].wait_ge(self.semobj[s], v)

    def _record(self, tok, reads, writes):
        for k in reads:
            self.readers.setdefault(k, []).append(tok)
        for k in writes:
            self.lastw[k] = tok
            self.readers[k] = []

    def op(self, e, fn, reads=(), writes=()):
        return self.group(e, [fn], reads, writes)

    def group(self, e, fns, reads=(), writes=()):
        if e == "pe":
            self.pe_labels.extend([self.phase] * len(fns))
        reads, writes = _flat(list(reads)), _flat(list(writes))
        pend = self._pending(e, self._deps(reads, writes))
        emb = pend.pop() if pend else None
        for s, v in pend:
            self.eng[e].wait_ge(self.semobj[s], v)
        ins = None
        for n, fn in enumerate(fns):
            ins = fn(self.eng[e])
            if n == 0 and emb is not None:
                ins._wait_ge(self.semobj[emb[0]], emb[1])
        self.cnt[e] += 1
        ins.then_inc(self.semobj[e], 1)
        tok = (e, self.cnt[e])
        self._record(tok, reads, writes)
        return tok

    def dma(self, e, semname, out, in_, reads=(), writes=()):
        reads, writes = _flat(list(reads)), _flat(list(writes))
        self._wait(e, self._deps(reads, writes))
        self.eng[e].dma_start(out=out, in_=in_).then_inc(self.semobj[semname], 16)
        self.dcnt[semname] = self.dcnt.get(semname, 0) + 16
        tok = (semname, self.dcnt[semname])
        self._record(tok, reads, writes)
        return tok

    def wait_keys(self, e, keys):
        keys = _flat(list(keys))
        toks = []
        for k in keys:
            t = self.lastw.get(k)
            if t is not None:
                toks.append(t)
            toks.extend(self.readers.get(k, ()))
        self._wait(e, toks)


def pg(off, nbytes):
    return [("pg", i) for i in range(off // PAGE, (off + nbytes - 1) // PAGE + 1)]


def piece_list():
    P = [("z", i) for i in range(5)]
    for mc in range(8):
        P.append(("g", mc))
    for q in range(4):
        P.append(("wvo", q))
    for nh in range(2):
        for r in range(2):
            P.append(("wo", nh, r))
    for q in range(16):
        P.append(("w1", q))
    for nh in range(2):
        for r in range(8):
            P.append(("w2", nh, r))
    return P


ZSETS = [[4, 5], [6, 7], [8], [0, 1], [2, 3]]
PIECES = piece_list()
NPIECE = len(PIECES)
HOST_PIECES = [i for i, p in enumerate(PIECES) if p[0] != "wvo"]
WVO_IDX = {p[1]: i for i, p in enumerate(PIECES) if p[0] == "wvo"}


def lhs_piece(W, colsets):
    K = W.shape[0]
    kc = K // 128
    out = np.zeros((128, 2048), np.float32)
    blocks = [W[:, cs].reshape(kc, 128, 128).transpose(1, 0, 2) for cs in colsets]
    arr = np.stack(blocks, axis=1).reshape(128, -1)
    out[:, : arr.shape[1]] = arr
    return out


def rhs_piece(W, rows0, col0):
    blk = W[rows0: rows0 + 512, col0: col0 + 512].reshape(4, 128, 512).transpose(1, 0, 2)
    return np.ascontiguousarray(blk).reshape(128, 2048)


def host_prep(inp):
    f = lambda a: np.asarray(a, np.float32)
    w_in = f(inp["w_in"])[0]
    ar = np.arange
    zc = [ar(0, 128), ar(128, 256), ar(256, 384), ar(384, 512),
          ar(512, 640), ar(640, 768),
          ar(768, 896),
          np.concatenate([ar(896, 960), ar(896, 960)]),
          np.concatenate([ar(928, 960), ar(896, 928), ar(928, 960), ar(896, 928)])]
    ga = [ar(960 + 128 * m, 960 + 128 * (m + 1)) for m in range(8)]
    gb = [ar(1984 + 128 * m, 1984 + 128 * (m + 1)) for m in range(8)]
    wpp = f(inp["w_pool_proj"])[0]
    w_out = f(inp["w_out"])[0]
    w1 = f(inp["w_mlp_in"])[0]
    w2 = f(inp["w_mlp_out"])[0]
    pieces = []
    for i in HOST_PIECES:
        p = PIECES[i]
        if p[0] == "z":
            pieces.append(lhs_piece(w_in, [zc[i] for i in ZSETS[p[1]]]))
        elif p[0] == "g":
            pieces.append(lhs_piece(w_in, [ga[p[1]], gb[p[1]]]))
        elif p[0] == "wo":
            pieces.append(rhs_piece(w_out, 512 * p[2], 512 * p[1]))
        elif p[0] == "w1":
            pieces.append(lhs_piece(w1, [ar(128 * (2 * p[1] + j), 128 * (2 * p[1] + j + 1)) for j in range(2)]))
        elif p[0] == "w2":
            pieces.append(rhs_piece(w2, 512 * p[2], 512 * p[1]))
    wst = np.stack(pieces, axis=0)
    wpph = np.concatenate([lhs_piece(wpp, [ar(128 * (4 * q + j), 128 * (4 * q + j + 1)) for j in range(4)])
                           for q in range(2)], axis=1)

    w_uq = f(inp["w_uq"])[0]
    w_ukv = f(inp["w_ukv"])[0]
    w_ap = f(inp["w_attn_proj"])[0]
    rope_cols, swap_cols = [], []
    for j in range(4):
        c, s = [], []
        for h in (2 * j, 2 * j + 1):
            b = h * 192 + 128
            c.append(ar(b, b + 64))
            s.append(np.concatenate([ar(b + 32, b + 64), ar(b, b + 32)]))
        rope_cols.append(np.concatenate(c))
        swap_cols.append(np.concatenate(s))
    wqr = lhs_piece(w_uq, rope_cols + swap_cols)
    A1 = np.stack([w_uq[:, h * 192: h * 192 + 128].T for h in range(8)], axis=1).reshape(128, 2048)
    A2 = np.stack([w_ukv[:, h * 256: h * 256 + 128].T for h in range(8)], axis=1).reshape(128, 1024)
    A3 = np.stack([w_ukv[:, h * 256 + 128: h * 256 + 256].T for h in range(8)], axis=1).reshape(128, 1024)
    A4 = np.ascontiguousarray(w_ap.reshape(8, 128, 1024).transpose(1, 0, 2)).reshape(128, 8192)
    poolw = np.ascontiguousarray(f(inp["pool_w"])[0].transpose(1, 0, 2)).reshape(128, 512)

    cols = np.zeros((128, NCOL), np.float32)
    cols[:, C_GMIX:C_GMIX + 8] = f(inp["g_mix"])[0].reshape(8, 128).T
    cols[:, C_GMLP:C_GMLP + 8] = f(inp["g_mlp"])[0].reshape(8, 128).T
    cols[:, C_PSC:C_PSC + 4] = f(inp["pool_scale"])[0].reshape(4, 128).T
    cols[:, C_GQ:C_GQ + 2] = f(inp["g_q"])[0].reshape(2, 128).T
    cols[:, C_GKV] = f(inp["g_kv"])[0]
    cols[:, C_BG:C_BG + 16] = f(inp["b_gate"])[0].reshape(16, 128).T
    half = 32
    inv_freq = (10000.0 ** (-np.arange(half, dtype=np.float32) / half)).astype(np.float32)
    p = np.arange(128)
    cols[:, C_INVF] = (inv_freq[p % 32].astype(np.float64) / (2 * np.pi)).astype(np.float32)
    twopi = 2 * np.pi * (1 - 1e-6)
    cols[:, C_SGN] = np.where((p % 64) < 32, -twopi, twopi).astype(np.float32)
    cols[:, C_EPS] = EPS
    cols[:, C_QUART] = 0.25
    ident = np.eye(128, dtype=np.float32)
    tri = (np.arange(128)[None, :] >= np.arange(128)[:, None]).astype(np.float32)
    invcnt = np.tile((1.0 / np.arange(1, 17, dtype=np.float32))[None, :], (128, 1)).astype(np.float32)
    return dict(wst=wst, wqr=wqr, A1=A1, A2=A2, A3=A3, A4=A4, poolw=poolw, cols=cols, ident=ident, tri=tri,
                invcnt=invcnt, wpph=wpph, gfin=f(inp["g_final"]).reshape(1, 1024))


def build(nseq=2, nch=8):
    S_LEN = nch * T
    NTOK = nseq * S_LEN
    NCHUNK = nseq * nch
    nc = bass.Bass("TRN2", target_bir_lowering=False)
    x_d = nc.dram_tensor("x", [NTOK, D], F32, kind="ExternalInput").ap()
    pos_d = nc.dram_tensor("pos", [1, NTOK], I32, kind="ExternalInput").ap()
    wst_d = nc.dram_tensor("wst", [len(HOST_PIECES), 128, 2048], F32, kind="ExternalInput").ap()
    wqr_d = nc.dram_tensor("wqr", [128, 2048], F32, kind="ExternalInput").ap()
    A1_d = nc.dram_tensor("A1", [128, 2048], F32, kind="ExternalInput").ap()
    A2_d = nc.dram_tensor("A2", [128, 1024], F32, kind="ExternalInput").ap()
    A3_d = nc.dram_tensor("A3", [128, 1024], F32, kind="ExternalInput").ap()
    A4_d = nc.dram_tensor("A4", [128, 8192], F32, kind="ExternalInput").ap()
    wpph_d = nc.dram_tensor("wpph", [128, 4096], F32, kind="ExternalInput").ap()
    poolw_d = nc.dram_tensor("poolw", [128, 512], F32, kind="ExternalInput").ap()
    cols_d = nc.dram_tensor("cols", [128, NCOL], F32, kind="ExternalInput").ap()
    ident_d = nc.dram_tensor("ident", [128, 128], F32, kind="ExternalInput").ap()
    tri_d = nc.dram_tensor("tri", [128, 128], F32, kind="ExternalInput").ap()
    invcnt_d = nc.dram_tensor("invcnt", [128, 16], F32, kind="ExternalInput").ap()
    gfin_d = nc.dram_tensor("gfin", [1, 1024], F32, kind="ExternalInput").ap()
    out_d = nc.dram_tensor("out", [NTOK, D], F32, kind="ExternalOutput").ap()
    wsc_d = nc.dram_tensor("wsc", [NPIECE, 128, 2048], BF16).ap()

    S = Sched(nc)
    U_UT, U_SAD, U_SBD, U_SAP, U_SBP = 0, 8448, 10560, 12672, 14784
    U_POOLED, U_CQ, U_CKV, U_KR, U_KRS, U_SQ, U_RSTD = 16896, 20992, 25088, 27136, 29184, 31232, 34304
    U_POSI, U_POSF, U_TS, U_TC = 38400, 40448, 42496, 44544
    U_SIZE = 47104
    U_FT, U_R = 0, 40448
    U_RT1, U_RT2 = U_SAD, U_SAD + 2048
    U_KT1, U_KT2 = U_SAP, U_SAP + 2048 + 64
    U_OT, U_YT, U_HS = 16896, 25088, 42496
    U_A4, U_STG, U_A1, U_A2, U_A3 = 0, 16384, 32768, 36864, 38912
    O_HT = U_SIZE
    O_MIX = O_HT + 8192
    O_CQN = O_MIX + 4096
    O_QABS = O_CQN + 2048
    O_QR = O_QABS + 8192
    O_PT = O_QR + 4096
    NPT = 6
    O_JUNK = O_PT + 1024 * NPT
    O_GT = O_JUNK + 2048
    A_SIZE = O_GT + 16384
    O_T1 = O_QABS
    U_ES, U_RS = 0, 4096
    U_ESP = 9216

    ctxs = []

    def sb(name, shape, dt):
        cm = nc.sbuf_tensor(name, shape, dt)
        t = cm.__enter__()
        ctxs.append(cm)
        return t

    def ps(name, shape, dt):
        cm = nc.psum_tensor(name, shape, dt)
        t = cm.__enter__()
        ctxs.append(cm)
        return t

    arena = sb("arena", [128, A_SIZE], U8)
    ring = sb("ring", [128, NSLOT * 2048], BF16)
    xb = sb("xb", [128, 2 * NT * D], F32)
    ckvnT = sb("ckvnT", [128, S_LEN], BF16)
    krlo = sb("krlo", [128, S_LEN], BF16)
    krhi = sb("krhi", [128, S_LEN], BF16)
    ckvtok = sb("ckvtok", [128, S_LEN], BF16)
    poolw = sb("poolw_s", [128, 512], BF16)
    wqr = sb("wqr_s", [128, 2048], BF16)
    wpps = sb("wpp_s", [128, 4096], BF16)
    wqa = sb("wqa_s", [128, 2048], BF16)
    identb = sb("identb", [128, 128], BF16)
    onesb = sb("onesb", [128, 128], BF16)
    onesf = sb("onesf", [128, 128], F32)
    trib = sb("trib", [128, 128], BF16)
    maskb = sb("maskb", [128, 128], BF16)
    cols = sb("cols_s", [128, NCOL], F32)
    bgh = sb("bgh", [128, 16], F32)
    invcnt = sb("invcnt_s", [128, 16], F32)
    gfin = sb("gfin_s", [128, 1024], F32)
    ssb = sb("ssb", [128, 48], F32)
    halo = sb("halo", [128, 64], F32)
    banks = [ps("bank%d" % i, [128, 512], F32) for i in range(8)]

    def AV(off, nbytes, dt):
        return arena[:, off:off + nbytes].bitcast(dt)

    def BK(b):
        return ("bank", b)

    uT = AV(U_UT, 8448, F32).rearrange("p (g t) -> p g t", g=4)
    sA = {"dve": AV(U_SAD, 2112, F32), "pool": AV(U_SAP, 2112, F32)}
    sB = {"dve": AV(U_SBD, 2112, F32), "pool": AV(U_SBP, 2112, F32)}
    sAk = {"dve": pg(U_SAD, 2112), "pool": pg(U_SAP, 2112)}
    sBk = {"dve": pg(U_SBD, 2112), "pool": pg(U_SBP, 2112)}
    pooledT = AV(U_POOLED, 4096, BF16).rearrange("p (g t) -> p g t", g=4)
    cq = AV(U_CQ, 4096, F32).rearrange("p (g t) -> p g t", g=2)
    ckv = AV(U_CKV, 2048, F32)
    kr = AV(U_KR, 2048, F32)
    krs = AV(U_KRS, 2048, F32)
    sq = AV(U_SQ, 3072, BF16).rearrange("p (g t) -> p g t", g=3)
    rstd = AV(U_RSTD, 4096, F32).rearrange("p (g t) -> p g t", g=2)
    posi = AV(U_POSI, 2048, I32)
    posf = AV(U_POSF, 2048, F32)
    ts_ = AV(U_TS, 2048, F32)
    tc_ = AV(U_TC, 2048, F32)
    fT = AV(U_FT, 32768, BF16).rearrange("p (g t) -> p g t", g=32)
    rbuf = [AV(U_R + 2048 * i, 2048, F32) for i in range(3)]
    rt1, rt2 = AV(U_RT1, 2048, F32), AV(U_RT2, 2048, F32)
    kt1, kt2 = AV(U_KT1, 2048, F32), AV(U_KT2, 2048, F32)
    oT = AV(U_OT, 8192, BF16).rearrange("p (g t) -> p g t", g=8)
    yT = AV(U_YT, 8192, BF16).rearrange("p (g t) -> p g t", g=8)
    HS_OFF = [43008, 45056, 40960, 34816]
    hs = [AV(o, 2048, BF16) for o in HS_OFF]
    hT = AV(O_HT, 8192, BF16).rearrange("p (g t) -> p g t", g=8)
    mixedT = AV(O_MIX, 4096, BF16).rearrange("p (g t) -> p g t", g=4)
    cqnT = AV(O_CQN, 2048, BF16).rearrange("p (g t) -> p g t", g=2)
    qabs = AV(O_QABS, 8192, BF16).rearrange("p (g t) -> p g t", g=8)
    qr = AV(O_QR, 4096, BF16).rearrange("p (g t) -> p g t", g=4)
    PT = [AV(O_PT + 1024 * i, 1024, BF16) for i in range(NPT)]
    rs = [AV(U_RS + 2048 * i, 2048, F32) for i in range(2)]
    esum = [AV(U_ES + 2048 * i, 2048, F32) for i in range(2)]
    esump = [AV(U_ESP + 2048 * i, 2048, F32) for i in range(2)]
    gts = AV(O_GT, 16384, BF16).rearrange("p (g t) -> p g t", g=16)
    junk = AV(O_JUNK, 2048, BF16)
    t12 = [[AV(O_T1 + 4096 * s + 2048 * i, 2048, F32) for i in range(2)] for s in range(2)]
    A4s = AV(U_A4, 16384, BF16).rearrange("p (h n) -> p h n", h=8)
    stg = AV(U_STG, 16384, BF16).rearrange("p (q j h m) -> p q j h m", q=4, j=2, h=8)
    A1s = AV(U_A1, 4096, BF16).rearrange("p (h r) -> p h r", h=8)
    A2s = AV(U_A2, 2048, BF16).rearrange("p (h c) -> p h c", h=8)
    A3s = AV(U_A3, 2048, BF16).rearrange("p (h c) -> p h c", h=8)
    xbv = xb[:].rearrange("p (b t d) -> p b t d", b=2, t=NT)
    ckvtokv = ckvtok[:].rearrange("p (b c) -> p b c", c=128)
    wqrv = wqr[:].rearrange("p (j k m) -> p j k m", j=8, k=2)
    wqav = wqa[:].rearrange("p (k h c) -> p k h c", k=2, h=8)
    poolwv = poolw[:].rearrange("p (g d) -> p g d", g=4)

    def col(i):
        return cols[:, i:i + 1]

    xsem = [S.new_sem("x%d" % b) for b in range(2)]
    osem = [S.new_sem("o%d" % b) for b in range(2)]
    psem = S.new_sem("pos")

    def load_x(ci):
        b = ci % 2
        S.dma("sp", xsem[b], xbv[:, b], x_d[ci * T:(ci + 1) * T, :].rearrange("(t p) d -> p t d", p=128),
              writes=[("x", b, tt) for tt in range(NT)])

    def load_pos(ci):
        S.dma("sp", psem, posi, pos_d[0:1, ci * T:(ci + 1) * T].partition_broadcast(128), writes=pg(U_POSI, 2048))

    load_x(0)
    if NCHUNK > 1:
        load_x(1)
    isem = [S.new_sem("i%d" % i) for i in range(16)]
    S.dma("sp", isem[0], cols[:], cols_d[:, :], writes=["cols"])
    S.dma("sp", isem[1], invcnt[:], invcnt_d[:, :], writes=["invcnt"])
    S.dma("sp", isem[2], gfin[:], gfin_d[0:1, :].partition_broadcast(128), writes=["gfin"])
    S.dma("pool", isem[3], identb[:], ident_d[:, :], writes=["ident"])
    S.dma("pool", isem[4], trib[:], tri_d[:, :], writes=["tri"])
    S.dma("pool", isem[5], poolw[:], poolw_d[:, :], writes=["poolw"])
    S.dma("pool", isem[6], wqr[:], wqr_d[:, :], writes=["wqr"])
    for q in range(2):
        S.dma("pool", isem[14 + q], wpps[:, 2048 * q:2048 * (q + 1)], wpph_d[:, 2048 * q:2048 * (q + 1)],
              writes=[("wpp", q)])
    S.dma("pool", isem[7], AV(U_A1, 4096, BF16), A1_d[:, :], writes=pg(U_A1, 4096))
    S.dma("pool", isem[8], AV(U_A2, 2048, BF16), A2_d[:, :], writes=pg(U_A2, 2048))
    S.dma("pool", isem[9], AV(U_A3, 2048, BF16), A3_d[:, :], writes=pg(U_A3, 2048))
    for i in range(4):
        S.dma("pool", isem[10 + i], AV(U_A4 + 4096 * i, 4096, BF16), A4_d[:, 2048 * i:2048 * (i + 1)],
              writes=pg(U_A4 + 4096 * i, 4096))
    for n, i in enumerate(HOST_PIECES):
        sname = S.new_sem("c%d" % i)
        S.dma("pool", sname, wsc_d[i], wst_d[n], reads=([("wsc", 12)] if n > 12 else []), writes=[("wsc", i)])
    S.op("dve", lambda e: e.memset(onesb[:], 1.0), writes=["ones"])
    S.op("dve", lambda e: e.tensor_scalar(out=maskb[:], in0=trib[:], scalar1=-1.0, scalar2=30000.0, op0=ALU.add,
                                           op1=ALU.mult), reads=["tri"], writes=["maskb"])
    S.op("dve", lambda e: e.memset(onesf[:], 1.0), writes=["onesf"])
    S.op("dve", lambda e: e.memset(krlo[:], 0.0), writes=[("krlo", c) for c in range(nch)])
    S.op("dve", lambda e: e.memset(krhi[:], 0.0), writes=[("krhi", c) for c in range(nch)])
    S.op("dve", lambda e: e.tensor_scalar(out=bgh[:], in0=cols[:, C_BG:C_BG + 16], scalar1=0.5, scalar2=None,
                                           op0=ALU.mult), reads=["cols"], writes=["bgh"])
    bctr = [0]

    def nb():
        b = bctr[0] % 8
        bctr[0] += 1
        return b

    for rc in range(2):
        for hg in range(2):
            b = nb()
            S.group("pe", [(lambda e, i=i, b=b, rc=rc, hg=hg: e.matmul(
                out=banks[b][:, i * 128:(i + 1) * 128], lhsT=A1s[:, hg * 4 + i, rc * 128:(rc + 1) * 128],
                rhs=A2s[:, hg * 4 + i, :], start=True, stop=True)) for i in range(4)],
                reads=[pg(U_A1, 4096), pg(U_A2, 2048)], writes=[BK(b)])
            S.op("dve", lambda e, b=b, rc=rc, hg=hg: e.tensor_copy(
                out=wqav[:, rc, hg * 4:(hg + 1) * 4, :],
                in_=banks[b][:].rearrange("p (h c) -> p h c", h=4)), writes=[BK(b), "wqa"])
    for h in range(8):
        for nh in range(2):
            b = nb()
            S.group("pe", [lambda e, b=b, h=h, nh=nh: e.matmul(out=banks[b][:], lhsT=A3s[:, h, :],
                                                                rhs=A4s[:, h, nh * 512:(nh + 1) * 512],
                                                                start=True, stop=True)],
                    reads=[pg(U_A3, 2048), pg(U_A4, 16384)], writes=[BK(b)])
            eng = "dve" if (h + nh) % 2 == 0 else "act"
            if eng == "dve":
                S.op("dve", lambda e, b=b, h=h, nh=nh: e.tensor_copy(
                    out=stg[:, 2 * nh:2 * nh + 2, :, h, :],
                    in_=banks[b][:].rearrange("p (q j m) -> p q j m", q=2, j=2)),
                    writes=[BK(b), pg(U_STG + 8192 * nh, 8192)])
            else:
                S.op("act", lambda e, b=b, h=h, nh=nh: e.activation(
                    out=stg[:, 2 * nh:2 * nh + 2, :, h, :],
                    in_=banks[b][:].rearrange("p (q j m) -> p q j m", q=2, j=2), func=AF.Copy),
                    writes=[BK(b), pg(U_STG + 8192 * nh, 8192)])
    for q in range(4):
        sname = S.new_sem("v%d" % q)
        S.dma("sp", sname, wsc_d[WVO_IDX[q]], AV(U_STG + 4096 * q, 4096, BF16),
              reads=pg(U_STG + 4096 * q, 4096), writes=[("wsc", WVO_IDX[q])])

    rsem = [S.new_sem("r%d" % s) for s in range(NSLOT)]
    rst = {"issued": 0, "acq": 0, "rel": 0}
    TOTAL = NCHUNK * NPIECE

    def ring_prefetch():
        while rst["issued"] < min(TOTAL, rst["rel"] + NSLOT):
            g = rst["issued"]
            s = g % NSLOT
            p = g % NPIECE
            S.dma("sp", rsem[s], ring[:, s * 2048:(s + 1) * 2048], wsc_d[p], reads=[("wsc", p)], writes=[("ring", s)])
            rst["issued"] += 1

    def ring_acquire(kind):
        g = rst["acq"]
        assert PIECES[g % NPIECE][0] == kind, (PIECES[g % NPIECE], kind)
        ring_prefetch()
        assert rst["issued"] > g
        rst["acq"] += 1
        s = g % NSLOT
        return s, ring[:, s * 2048:(s + 1) * 2048]

    def ring_release():
        rst["rel"] += 1
        ring_prefetch()

    def rmsnorm_to_hT(ci, gcol, tb):
        b = ci % 2
        for tt in range(NT):
            X = xbv[:, b, tt, :]
            sc = ssb[:, 32 + 4 * tt:32 + 4 * tt + 4]
            sk = ("ss", tt)
            S.op("act", lambda e, X=X, sc=sc: e.activation(out=junk, in_=X, func=AF.Square, accum_out=sc[:, 0:1]),
                 reads=[("x", b, tt)], writes=[pg(O_JUNK, 2048), sk])
            S.op("act", lambda e, sc=sc: e.activation(out=sc[:, 1:2], in_=sc[:, 0:1], func=AF.Ln, scale=1.0 / D,
                                                      bias=col(C_EPS)), reads=["cols"], writes=[sk])
            S.op("act", lambda e, sc=sc: e.activation(out=sc[:, 2:3], in_=sc[:, 1:2], func=AF.Exp, scale=-0.5),
                 writes=[sk])
            h_ = hs[tt]
            hk = pg(HS_OFF[tt], 2048)
            S.op("dve", lambda e, X=X, sc=sc, h_=h_: e.tensor_scalar(out=h_, in0=X, scalar1=sc[:, 2:3], scalar2=None,
                                                                      op0=ALU.mult),
                 reads=[("x", b, tt), sk], writes=[hk])
            S.group("pe", [(lambda e, kc=kc, h_=h_, tt=tt: e.transpose(
                out=banks[tb[kc // 2]][:].bitcast(BF16)[:, (kc % 2) * 512 + tt * 128:(kc % 2) * 512 + (tt + 1) * 128],
                in_=h_[:, kc * 128:(kc + 1) * 128], identity=identb[:])) for kc in range(8)],
                reads=[hk, "ident"], writes=[BK(tb[i]) for i in range(4)])
        for kc in range(8):
            src = banks[tb[kc // 2]][:].bitcast(BF16)[:, (kc % 2) * 512:(kc % 2) * 512 + 512]
            if (kc // 2) % 2 == 0:
                S.op("act", lambda e, kc=kc, src=src: e.activation(out=hT[:, kc, :], in_=src, func=AF.Copy,
                                                                   scale=col(gcol + kc)),
                     reads=["cols"], writes=[BK(tb[kc // 2]), pg(O_HT + 1024 * kc, 1024)])
            else:
                S.op("dve", lambda e, kc=kc, src=src: e.tensor_scalar(out=hT[:, kc, :], in0=src, scalar1=col(gcol + kc),
                                                                      scalar2=None, op0=ALU.mult),
                     reads=["cols"], writes=[BK(tb[kc // 2]), pg(O_HT + 1024 * kc, 1024)])

    U_HS2 = 0
    hs2 = [AV(U_HS2 + 2048 * i, 2048, BF16) for i in range(4)]

    def n2_stats(ci, tt):
        b = ci % 2
        X = xbv[:, b, tt, :]
        sc = ssb[:, 16 + 4 * tt:16 + 4 * tt + 4]
        sk = ("ss2", tt)
        S.op("act", lambda e, X=X, sc=sc: e.activation(out=junk, in_=X, func=AF.Square, accum_out=sc[:, 0:1]),
             reads=[("x", b, tt)], writes=[pg(O_JUNK, 2048), sk])
        S.op("act", lambda e, sc=sc: e.activation(out=sc[:, 1:2], in_=sc[:, 0:1], func=AF.Ln, scale=1.0 / D,
                                                  bias=col(C_EPS)), reads=["cols"], writes=[sk])
        S.op("act", lambda e, sc=sc: e.activation(out=sc[:, 2:3], in_=sc[:, 1:2], func=AF.Exp, scale=-0.5),
             writes=[sk])

    def n2_scale(ci, tt):
        b = ci % 2
        X = xbv[:, b, tt, :]
        sc = ssb[:, 16 + 4 * tt:16 + 4 * tt + 4]
        S.op("dve", lambda e, X=X, sc=sc, h_=hs2[tt]: e.tensor_scalar(out=h_, in0=X, scalar1=sc[:, 2:3], scalar2=None,
                                                                      op0=ALU.mult),
             reads=[("x", b, tt), ("ss2", tt)], writes=[pg(U_HS2 + 2048 * tt, 2048)])

    def n2_transposes(tt):
        h_ = hs2[tt]
        S.group("pe", [(lambda e, kc=kc, h_=h_, tt=tt: e.transpose(
            out=banks[kc // 2][:].bitcast(BF16)[:, (kc % 2) * 512 + tt * 128:(kc % 2) * 512 + (tt + 1) * 128],
            in_=h_[:, kc * 128:(kc + 1) * 128], identity=identb[:])) for kc in range(8)],
            reads=[pg(U_HS2 + 2048 * tt, 2048), "ident"], writes=[BK(i) for i in range(4)])

    def n2_evacs(gcol):
        for kc in range(8):
            src = banks[kc // 2][:].bitcast(BF16)[:, (kc % 2) * 512:(kc % 2) * 512 + 512]
            if kc in (0, 3, 6):
                S.op("act", lambda e, kc=kc, src=src: e.activation(out=hT[:, kc, :], in_=src, func=AF.Copy,
                                                                   scale=col(gcol + kc)),
                     reads=["cols"], writes=[BK(kc // 2), pg(O_HT + 1024 * kc, 1024)])
            else:
                S.op("dve", lambda e, kc=kc, src=src: e.tensor_scalar(out=hT[:, kc, :], in0=src, scalar1=col(gcol + kc),
                                                                      scalar2=None, op0=ALU.mult),
                     reads=["cols"], writes=[BK(kc // 2), pg(O_HT + 1024 * kc, 1024)])

    HTK = pg(O_HT, 8192)
    evt = [0]

    def evac_copy(dst, dkeys, b, scale_ap=None, extra_reads=()):
        evt[0] += 1
        if evt[0] % 2 == 0:
            if scale_ap is None:
                S.op("act", lambda e: e.activation(out=dst, in_=banks[b][:], func=AF.Copy), reads=list(extra_reads),
                     writes=[BK(b), dkeys])
            else:
                S.op("act", lambda e: e.activation(out=dst, in_=banks[b][:], func=AF.Copy, scale=scale_ap),
                     reads=list(extra_reads), writes=[BK(b), dkeys])
        else:
            if scale_ap is None:
                S.op("dve", lambda e: e.tensor_copy(out=dst, in_=banks[b][:]), reads=list(extra_reads),
                     writes=[BK(b), dkeys])
            else:
                S.op("dve", lambda e: e.tensor_scalar(out=dst, in0=banks[b][:], scalar1=scale_ap, scalar2=None,
                                                      op0=ALU.mult), reads=list(extra_reads), writes=[BK(b), dkeys])

    zdst = [(uT[:, g, 16:528], pg(U_UT + 2112 * g, 2112)) for g in range(4)] + \
           [(cq[:, 0, :], pg(U_CQ, 2048)), (cq[:, 1, :], pg(U_CQ + 2048, 2048)),
            (ckv, pg(U_CKV, 2048)), (kr, pg(U_KR, 2048)), (krs, pg(U_KRS, 2048))]

    def z_piece(zp):
        s, rp = ring_acquire("z")
        rv = rp.rearrange("p (j k m) -> p j k m", j=2, k=8)
        for j, zi in enumerate(ZSETS[zp]):
            b = nb()
            S.group("pe", [(lambda e, kc=kc, j=j, rv=rv, b=b: e.matmul(out=banks[b][:], lhsT=rv[:, j, kc, :],
                                                                       rhs=hT[:, kc, :], start=(kc == 0),
                                                                       stop=(kc == 7))) for kc in range(8)],
                    reads=[("ring", s), HTK], writes=[BK(b)])
            evac_copy(zdst[zi][0], zdst[zi][1], b)
        ring_release()

    def pool_group(c, g):
        e_ = "dve" if g < 2 else "pool"
        w = 2 ** (g + 1)
        uk = pg(U_UT + 2112 * g, 2112)
        src, srck = uT[:, g, :], uk
        bufs = [(sA[e_], sAk[e_]), (sB[e_], sBk[e_])]
        for k in range(g + 1):
            sh = 2 ** k
            lo = 2 ** (k + 1) - 1
            dst, dstk = bufs[k % 2]
            S.op(e_, lambda e, src=src, dst=dst, sh=sh, lo=lo: e.tensor_tensor(
                out=dst[:, lo:528], in0=src[:, lo:528], in1=src[:, lo - sh:528 - sh], op=ALU.add),
                reads=[srck], writes=[dstk])
            src, srck = dst, dstk
        S.op("dve", lambda e, src=src, g=g, w=w: e.scalar_tensor_tensor(
            out=pooledT[:, g, :], in0=src[:, 16:528], scalar=1.0 / w, in1=uT[:, g, 16:528],
            op0=ALU.mult, op1=ALU.subtract), reads=[srck, uk], writes=pg(U_POOLED + 1024 * g, 1024))
        if c == 0:
            other = bufs[(g + 1) % 2]
            S.op(e_, lambda e, src=src, other=other, w=w: e.tensor_tensor(
                out=other[0][:, 0:w - 1], in0=src[:, 16:16 + w - 1], in1=invcnt[:, 0:w - 1], op=ALU.mult),
                reads=[srck, "invcnt"], writes=[other[1]])
            S.op(e_, lambda e, other=other, g=g, w=w: e.tensor_tensor(
                out=pooledT[:, g, 0:w - 1], in0=other[0][:, 0:w - 1], in1=uT[:, g, 16:16 + w - 1],
                op=ALU.subtract), reads=[other[1], uk], writes=pg(U_POOLED + 1024 * g, 1024))
        S.op(e_, lambda e, g=g: e.tensor_copy(out=halo[:, 16 * g:16 * (g + 1)], in_=uT[:, g, 512:528]),
             reads=[uk], writes=[("halo", g)])

    def pool_linear(g):
        b = nb()
        S.group("pe", [lambda e, g=g, b=b: e.matmul(out=banks[b][:], lhsT=poolwv[:, g, :], rhs=pooledT[:, g, :],
                                                    start=True, stop=True)],
                reads=["poolw", pg(U_POOLED + 1024 * g, 1024)], writes=[BK(b)])
        evac_copy(mixedT[:, g, :], pg(O_MIX + 1024 * g, 1024), b, scale_ap=col(C_PSC + g), extra_reads=["cols"])

    def gate_pre(mc):
        s, rp = ring_acquire("g")
        gv = rp.rearrange("p (j k m) -> p j k m", j=2, k=8)
        for i in range(2):
            b = nb()
            S.group("pe", [(lambda e, kc=kc, i=i, b=b: e.matmul(out=banks[b][:], lhsT=gv[:, i, kc, :],
                                                                rhs=hT[:, kc, :], start=(kc == 0), stop=(kc == 7)))
                           for kc in range(8)], reads=[("ring", s), HTK], writes=[BK(b)])
            S.op("act", lambda e, i=i, b=b: e.activation(out=gts[:, 8 * i + mc, :], in_=banks[b][:], func=AF.Tanh,
                                                         scale=0.5, bias=bgh[:, 8 * i + mc:8 * i + mc + 1]),
                 reads=["bgh"], writes=[BK(b), pg(O_GT + 1024 * (8 * i + mc), 1024)])
        ring_release()

    SINK, COSK = pg(U_TS, 2048), pg(U_TC, 2048)
    CQNK = pg(O_CQN, 2048)

    def stream_A(ci):
        c = ci % nch
        S.op("act", lambda e: e.activation(out=sq[:, 0:2, :], in_=cq[:, :, :], func=AF.Square),
             reads=pg(U_CQ, 4096), writes=pg(U_SQ, 2048))
        S.op("act", lambda e: e.activation(out=sq[:, 2, :], in_=ckv, func=AF.Square),
             reads=pg(U_CKV, 2048), writes=pg(U_SQ + 2048, 1024))
        S.op("dve", lambda e: e.tensor_copy(out=posf, in_=posi), reads=pg(U_POSI, 2048), writes=pg(U_POSF, 2048))
        S.op("pool", lambda e: e.tensor_scalar(out=ts_, in0=posf, scalar1=col(C_INVF), scalar2=0.0, op0=ALU.mult,
                                               op1=ALU.add),
             reads=[pg(U_POSF, 2048), "cols"], writes=pg(U_TS, 2048))
        S.op("pool", lambda e: e.tensor_scalar(out=tc_, in0=ts_, scalar1=1.0, scalar2=0.25, op0=ALU.mult,
                                               op1=ALU.add),
             reads=pg(U_TS, 2048), writes=pg(U_TC, 2048))
        yield
        bq, bkv = nb(), nb()
        S.group("pe", [(lambda e, i=i: e.matmul(out=banks[bq][:], lhsT=onesb[:], rhs=sq[:, i, :], start=(i == 0),
                                                stop=(i == 1))) for i in range(2)],
                reads=["ones", pg(U_SQ, 2048)], writes=[BK(bq)])
        S.group("pe", [lambda e: e.matmul(out=banks[bkv][:], lhsT=onesb[:], rhs=sq[:, 2, :], start=True, stop=True)],
                reads=["ones", pg(U_SQ + 2048, 1024)], writes=[BK(bkv)])
        for i, (bb, n) in enumerate(((bq, 256), (bkv, 128))):
            rk = pg(U_RSTD + 2048 * i, 2048)
            S.op("act", lambda e, i=i, bb=bb, n=n: e.activation(out=rstd[:, i, :], in_=banks[bb][:], func=AF.Ln,
                                                                scale=1.0 / n, bias=col(C_EPS)),
                 reads=["cols"], writes=[BK(bb), rk])
            S.op("act", lambda e, i=i: e.activation(out=rstd[:, i, :], in_=rstd[:, i, :], func=AF.Exp, scale=-0.5),
                 writes=[rk])
        for (tv, tk) in ((ts_, pg(U_TS, 2048)), (tc_, pg(U_TC, 2048))):
            S.op("dve", lambda e, tv=tv: e.tensor_copy(out=posi, in_=tv), reads=[tk], writes=pg(U_POSI, 2048))
            S.op("dve", lambda e: e.tensor_copy(out=posf, in_=posi), reads=pg(U_POSI, 2048), writes=pg(U_POSF, 2048))
            S.op("pool", lambda e, tv=tv: e.tensor_tensor(out=tv, in0=tv, in1=posf, op=ALU.subtract),
                 reads=pg(U_POSF, 2048), writes=[tk])
        yield
        for rc in range(2):
            S.op("dve", lambda e, rc=rc: e.scalar_tensor_tensor(out=cqnT[:, rc, :], in0=cq[:, rc, :],
                                                                scalar=col(C_GQ + rc), in1=rstd[:, 0, :],
                                                                op0=ALU.mult, op1=ALU.mult),
                 reads=[pg(U_CQ + 2048 * rc, 2048), pg(U_RSTD, 2048), "cols"], writes=pg(O_CQN + 1024 * rc, 1024))
        S.op("dve", lambda e: e.scalar_tensor_tensor(out=ckvnT[:, c * T:(c + 1) * T], in0=ckv, scalar=col(C_GKV),
                                                     in1=rstd[:, 1, :], op0=ALU.mult, op1=ALU.mult),
             reads=[pg(U_CKV, 2048), pg(U_RSTD + 2048, 2048), "cols"], writes=[("ckvnT", c)])
        S.op("act", lambda e: e.activation(out=ts_, in_=ts_, func=AF.Sin, scale=col(C_SGN)), reads=["cols"],
             writes=pg(U_TS, 2048))
        S.op("act", lambda e: e.activation(out=tc_, in_=tc_, func=AF.Sin, scale=float(2 * np.pi * (1 - 1e-6))),
             writes=pg(U_TC, 2048))
        yield
        bt = nb()
        S.group("pe", [(lambda e, i=i: e.transpose(out=banks[bt][:].bitcast(BF16)[:, i * 128:(i + 1) * 128],
                                                   in_=ckvnT[:, c * T + i * 128:c * T + (i + 1) * 128],
                                                   identity=identb[:])) for i in range(4)],
                reads=[("ckvnT", c), "ident"], writes=[BK(bt)])
        S.op("dve", lambda e: e.tensor_copy(out=ckvtok[:, c * T:(c + 1) * T], in_=banks[bt][:].bitcast(BF16)[:, 0:512]),
             writes=[BK(bt), ("ckvtok", c)])
        S.op("dve", lambda e: e.tensor_tensor(out=kt1, in0=kr, in1=tc_, op=ALU.mult), reads=[pg(U_KR, 2048), COSK],
             writes=pg(U_KT1, 2048))
        S.op("pool", lambda e: e.tensor_tensor(out=kt2, in0=krs, in1=ts_, op=ALU.mult), reads=[pg(U_KRS, 2048), SINK],
             writes=pg(U_KT2, 2048))
        S.op("dve", lambda e: e.tensor_tensor(out=krlo[0:64, c * T:(c + 1) * T], in0=kt1[0:64, :], in1=kt2[0:64, :],
                                              op=ALU.add), reads=[pg(U_KT1, 2048), pg(U_KT2, 2048)],
             writes=[("krlo", c)])
        S.op("dve", lambda e: e.tensor_tensor(out=krhi[64:128, c * T:(c + 1) * T], in0=kt1[64:128, :],
                                              in1=kt2[64:128, :], op=ALU.add),
             reads=[pg(U_KT1, 2048), pg(U_KT2, 2048)], writes=[("krhi", c)])
        for h in range(NH):
            b = nb()
            S.group("pe", [(lambda e, rc=rc, h=h, b=b: e.matmul(out=banks[b][:], lhsT=wqav[:, rc, h, :],
                                                                rhs=cqnT[:, rc, :], start=(rc == 0), stop=(rc == 1)))
                           for rc in range(2)], reads=["wqa", CQNK], writes=[BK(b)])
            evac_copy(qabs[:, h, :], pg(O_QABS + 1024 * h, 1024), b)
        yield
        for j in range(4):
            b1, b2 = nb(), nb()
            S.group("pe", [(lambda e, rc=rc, j=j: e.matmul(out=banks[b1][:], lhsT=wqrv[:, j, rc, :], rhs=cqnT[:, rc, :],
                                                           start=(rc == 0), stop=(rc == 1))) for rc in range(2)],
                    reads=["wqr", CQNK], writes=[BK(b1)])
            S.group("pe", [(lambda e, rc=rc, j=j: e.matmul(out=banks[b2][:], lhsT=wqrv[:, 4 + j, rc, :],
                                                           rhs=cqnT[:, rc, :], start=(rc == 0), stop=(rc == 1)))
                           for rc in range(2)], reads=["wqr", CQNK], writes=[BK(b2)])
            S.op("dve", lambda e, b1=b1: e.tensor_tensor(out=rt1, in0=banks[b1][:], in1=tc_, op=ALU.mult),
                 reads=[COSK], writes=[BK(b1), pg(U_RT1, 2048)])
            S.op("dve", lambda e, b2=b2: e.tensor_tensor(out=rt2, in0=banks[b2][:], in1=ts_, op=ALU.mult),
                 reads=[SINK], writes=[BK(b2), pg(U_RT2, 2048)])
            S.op("pool", lambda e, j=j: e.tensor_tensor(out=qr[:, j, :], in0=rt1, in1=rt2, op=ALU.add),
                 reads=[pg(U_RT1, 2048), pg(U_RT2, 2048)], writes=pg(O_QR + 1024 * j, 1024))
            if j % 2 == 1:
                yield

    def stream_B(ci):
        c = ci % nch
        z_piece(3)
        z_piece(4)
        pool_group(c, 2)
        pool_group(c, 3)
        yield
        gate_pre(0)
        gate_pre(1)
        yield
        gate_pre(2)
        gate_pre(3)
        gate_pre(4)
        pool_group(c, 0)
        pool_group(c, 1)
        yield
        gate_pre(5)
        gate_pre(6)
        yield
        gate_pre(7)
        yield
        for g in range(4):
            pool_linear(g)
        yield

    SBANKS = [0, 1, 2, 7]
    LOOK = 2

    def attention(ci):
        c = ci % nch
        nkb = 4 * c + 4
        n_o = 4 * c
        korder = []
        for i in range(4):
            korder.extend(range((i * n_o) // 4, ((i + 1) * n_o) // 4))
            korder.append(n_o + i)
        assert sorted(korder) == list(range(nkb)) and korder[0] == 0
        units = [(h, kb) for h in range(NH) for kb in korder]
        KFIRST, KLAST = korder[0], korder[-1]
        NU = len(units)

        def emit_S(u):
            h, kb = units[u]
            off = max(0, kb - 4 * c) * 128
            sbk = SBANKS[u % 4]
            krc = krlo if h % 2 == 0 else krhi
            krk = ("krlo" if h % 2 == 0 else "krhi", kb // 4)
            fns = [lambda e: e.matmul(out=banks[sbk][:, off:512], lhsT=ckvnT[:, kb * 128:(kb + 1) * 128],
                                      rhs=qabs[:, h, off:512], start=True, stop=False)]
            if kb >= 4 * c:
                fns.append(lambda e: e.matmul(out=banks[sbk][:, off:off + 128], lhsT=identb[:], rhs=maskb[:],
                                              start=False, stop=False))
            fns.append(lambda e: e.matmul(out=banks[sbk][:, off:512], lhsT=krc[:, kb * 128:(kb + 1) * 128],
                                          rhs=qr[:, h // 2, off:512], start=False, stop=True))
            S.group("pe", fns,
                    reads=[("ckvnT", kb // 4), krk, pg(O_QABS + 1024 * h, 1024), pg(O_QR + 1024 * (h // 2), 1024),
                           "ident", "maskb"],
                    writes=[BK(sbk)])

        D1, D2 = min(6, nkb - 3), min(8, nkb - 1)
        pend = []
        PESUM = 10
        pesum_n = [0] * NH

        def fin_pe(h):
            sb_ = 5 + (h % 2)
            es = esum[h % 2]
            esk = pg(U_ES + 2048 * (h % 2), 2048)
            S.phase = "ATTF"
            first = pesum_n[h] == 0
            S.group("pe", [lambda e: e.matmul(out=banks[sb_][:], lhsT=onesf[:], rhs=es, start=first, stop=True)],
                    reads=["onesf", esk], writes=[BK(sb_)])
            S.phase = "ATT"

        def fin_rest(h):
            ob, sb_ = 3 + (h % 2), 5 + (h % 2)
            r_ = rs[h % 2]
            rk = pg(U_RS + 2048 * (h % 2), 2048)
            S.op("act", lambda e: e.activation(out=r_, in_=banks[sb_][:], func=AF.Ln), writes=[BK(sb_), rk])
            S.op("act", lambda e: e.activation(out=r_, in_=r_, func=AF.Exp, scale=-1.0), writes=[rk])
            S.op("dve", lambda e: e.tensor_tensor(out=oT[:, h, :], in0=banks[ob][:], in1=r_, op=ALU.mult),
                 reads=[rk], writes=[BK(ob), pg(U_OT + 1024 * h, 1024)])

        for u in range(min(LOOK, NU)):
            emit_S(u)
        for u in range(NU):
            h, kb = units[u]
            off = max(0, kb - 4 * c) * 128
            if u + LOOK < NU:
                emit_S(u + LOOK)
            sbk = SBANKS[u % 4]
            pt = PT[u % NPT]
            ptk = pg(O_PT + 1024 * (u % NPT), 1024)
            S.op("act", lambda e, sbk=sbk, pt=pt, off=off: e.activation(out=pt[:, off:512], in_=banks[sbk][:, off:512],
                                                                        func=AF.Exp, scale=SCALE),
                 writes=[BK(sbk), ptk])
            ob, sb_ = 3 + (h % 2), 5 + (h % 2)
            es = esum[h % 2]
            esk = pg(U_ES + 2048 * (h % 2), 2048)
            S.group("pe", [
                lambda e, ob=ob, pt=pt, off=off, kb=kb: e.matmul(out=banks[ob][:, off:512], lhsT=ckvtokv[:, kb, :],
                                                                 rhs=pt[:, off:512], start=(kb == KFIRST),
                                                                 stop=(kb == KLAST))],
                reads=[("ckvtok", kb // 4), ptk], writes=[BK(ob)])
            if kb < 4 * c and kb % PESUM == PESUM - 1:
                first = pesum_n[h] == 0
                pesum_n[h] += 1
                S.group("pe", [lambda e, sb_=sb_, pt=pt, first=first: e.matmul(out=banks[sb_][:], lhsT=onesb[:], rhs=pt,
                                                                               start=first, stop=False)],
                        reads=["ones", ptk], writes=[BK(sb_)])
            elif kb == KFIRST:
                S.op("dve", lambda e, es=es, pt=pt: e.tensor_copy(out=es, in_=pt), reads=[ptk], writes=[esk])
            else:
                S.op("dve", lambda e, es=es, pt=pt, off=off: e.tensor_tensor(out=es[:, off:512], in0=es[:, off:512],
                                                                            in1=pt[:, off:512], op=ALU.add),
                     reads=[ptk], writes=[esk])
            for p in list(pend):
                if p[2] == 0 and u - p[1] >= D1:
                    fin_pe(p[0])
                    p[2] = 1
                elif p[2] == 1 and u - p[1] >= D2:
                    fin_rest(p[0])
                    pend.remove(p)
            if kb == KLAST:
                for p in list(pend):
                    if p[2] == 0:
                        fin_pe(p[0])
                    fin_rest(p[0])
                    pend.remove(p)
                pend.append([h, u, 0])
        for p in list(pend):
            if p[2] == 0:
                fin_pe(p[0])
            fin_rest(p[0])

    def gate_phase(ci):
        MIXK, OTK = pg(O_MIX, 4096), pg(U_OT, 8192)
        wvo_s = None
        AAB, BMB = [0, 2, 7, 1], [3, 4, 5, 6]

        def aa(mc):
            wppv = wpps[:, 2048 * (mc // 4):2048 * (mc // 4 + 1)].rearrange("p (j k m) -> p j k m", j=4, k=4)
            b = AAB[mc % 4]
            S.group("pe", [(lambda e, kc=kc: e.matmul(out=banks[b][:], lhsT=wppv[:, mc % 4, kc, :],
                                                      rhs=mixedT[:, kc, :], start=(kc == 0), stop=(kc == 3)))
                           for kc in range(4)], reads=[("wpp", mc // 4), MIXK], writes=[BK(b)])

        for mc in range(4):
            aa(mc)
        for mc in range(8):
            if mc % 2 == 0:
                wvo_s = ring_acquire("wvo")
            wvov = wvo_s[1].rearrange("p (j h m) -> p j h m", j=2, h=8)
            bs = [AAB[mc % 4], BMB[mc % 4]]
            S.group("pe", [(lambda e, h=h: e.matmul(out=banks[bs[1]][:], lhsT=wvov[:, mc % 2, h, :], rhs=oT[:, h, :],
                                                    start=(h == 0), stop=(h == 7))) for h in range(8)],
                    reads=[("ring", wvo_s[0]), OTK], writes=[BK(bs[1])])
            st = mc % 2
            for i in range(2):
                S.op("dve", lambda e, i=i: e.scalar_tensor_tensor(out=t12[st][i], in0=gts[:, 8 * i + mc, :], scalar=1.0,
                                                                  in1=banks[bs[i]][:], op0=ALU.add, op1=ALU.mult),
                     reads=pg(O_GT + 1024 * (8 * i + mc), 1024),
                     writes=[BK(bs[i]), pg(O_T1 + 4096 * st + 2048 * i, 2048)])
            S.op("pool", lambda e: e.tensor_tensor(out=yT[:, mc, :], in0=t12[st][0], in1=t12[st][1], op=ALU.add),
                 reads=pg(O_T1 + 4096 * st, 4096), writes=pg(U_YT + 1024 * mc, 1024))
            if mc % 2 == 1:
                ring_release()
            if mc + 4 < 8:
                aa(mc + 4)

    def emit_store(ci):
        b = ci % 2
        S.dma("sp", osem[b], out_d[ci * T:(ci + 1) * T, :].rearrange("(t p) d -> p t d", p=128), xbv[:, b],
              reads=[("x", b, tt) for tt in range(NT)])

    load_pos(0)
    S.phase = "N1"
    rmsnorm_to_hT(0, C_GMIX, [nb() for _ in range(4)])
    for ci in range(NCHUNK):
        c = ci % nch
        xbk = ci % 2
        XK = [("x", xbk, tt) for tt in range(NT)]
        S.phase = "Z"
        bctr[0] = 0
        if c == 0:
            S.op("pool", lambda e: e.memset(uT[:, :, 0:16], 0.0), writes=pg(U_UT, 8448))
        else:
            S.op("pool", lambda e: e.tensor_copy(out=uT[:, :, 0:16], in_=halo[:].rearrange("p (g t) -> p g t", g=4)),
                 reads=[("halo", g) for g in range(4)], writes=pg(U_UT, 8448))
        for zp in range(3):
            z_piece(zp)
        S.phase = "MIX"
        gens = [stream_A(ci), stream_B(ci)]
        alive = [True, True]
        while any(alive):
            for gi in range(2):
                if alive[gi]:
                    try:
                        next(gens[gi])
                    except StopIteration:
                        alive[gi] = False
        if ci >= 1:
            emit_store(ci - 1)
            if ci + 1 < NCHUNK:
                load_x(ci + 1)
        if ci + 1 < NCHUNK:
            load_pos(ci + 1)
        S.phase = "ATT"
        attention(ci)
        S.phase = "GATE"
        gate_phase(ci)
        S.phase = "WOUT"
        YTK = pg(U_YT, 8192)
        for nh in range(2):
            pcs = [ring_acquire("wo"), ring_acquire("wo")]
            def wo_mm(r, tt):
                pv = pcs[r][1].rearrange("p (k n) -> p k n", k=4)
                b = 4 * nh + tt
                S.group("pe", [(lambda e, k4=k4, pv=pv, b=b, tt=tt, r=r: e.matmul(
                    out=banks[b][:], lhsT=yT[:, 4 * r + k4, tt * 128:(tt + 1) * 128], rhs=pv[:, k4, :],
                    start=(r == 0 and k4 == 0), stop=(r == 1 and k4 == 3))) for k4 in range(4)],
                    reads=[("ring", pcs[r][0]), pg(U_YT + 4096 * r, 4096)], writes=[BK(b)])

            def wo_add(tt):
                b = 4 * nh + tt
                Xh = xbv[:, xbk, tt, nh * 512:(nh + 1) * 512]
                S.op("dve", lambda e, b=b, Xh=Xh: e.scalar_tensor_tensor(out=Xh, in0=banks[b][:], scalar=0.5, in1=Xh,
                                                                         op0=ALU.mult, op1=ALU.add),
                     writes=[BK(b), ("x", xbk, tt)])

            if nh == 0:
                for r in range(2):
                    for tt in range(NT):
                        wo_mm(r, tt)
                ring_release()
                ring_release()
                for tt in range(NT):
                    wo_add(tt)
            else:
                for tt in range(NT):
                    S.phase = "WOUT"
                    for r in range(2):
                        wo_mm(r, tt)
                    wo_add(tt)
                    S.phase = "N2"
                    n2_stats(ci, tt)
                    if tt >= 1:
                        n2_scale(ci, tt - 1)
                    if tt >= 2:
                        n2_transposes(tt - 2)
                ring_release()
                ring_release()
                n2_scale(ci, NT - 1)
                n2_transposes(NT - 2)
                n2_transposes(NT - 1)
                n2_evacs(C_GMLP)
        S.phase = "MLP1"
        bctr[0] = 4
        for q in range(16):
            s, rp = ring_acquire("w1")
            rv = rp.rearrange("p (j k m) -> p j k m", j=2, k=8)
            for j in range(2):
                fc = 2 * q + j
                b = nb()
                if fc == 0:
                    for kc in range(8):
                        S.group("pe", [lambda e, kc=kc, j=j, rv=rv, b=b: e.matmul(
                            out=banks[b][:], lhsT=rv[:, j, kc, :], rhs=hT[:, kc, :], start=(kc == 0), stop=(kc == 7))],
                            reads=[("ring", s), pg(O_HT + 1024 * kc, 1024)], writes=[BK(b)])
                else:
                    S.group("pe", [(lambda e, kc=kc, j=j, rv=rv, b=b: e.matmul(out=banks[b][:], lhsT=rv[:, j, kc, :],
                                                                               rhs=hT[:, kc, :], start=(kc == 0),
                                                                               stop=(kc == 7))) for kc in range(8)],
                            reads=[("ring", s), HTK], writes=[BK(b)])
                rb = rbuf[fc % 3]
                rk = pg(U_R + 2048 * (fc % 3), 2048)
                S.op("act", lambda e, b=b, rb=rb: e.activation(out=rb, in_=banks[b][:], func=AF.Relu),
                     writes=[BK(b), rk])
                e2 = "pool" if fc % 3 == 2 else "dve"
                S.op(e2, lambda e, rb=rb, fc=fc: e.tensor_tensor(out=fT[:, fc, :], in0=rb, in1=rb, op=ALU.mult),
                     reads=[rk], writes=pg(U_FT + 1024 * fc, 1024))
            ring_release()
        S.phase = "MLP2"
        for nh in range(2):
            for r in range(8):
                s, rp = ring_acquire("w2")
                pv = rp.rearrange("p (k n) -> p k n", k=4)
                for tt in range(NT):
                    b = 4 * nh + tt
                    S.group("pe", [(lambda e, k4=k4, pv=pv, b=b, tt=tt, r=r: e.matmul(
                        out=banks[b][:], lhsT=fT[:, 4 * r + k4, tt * 128:(tt + 1) * 128], rhs=pv[:, k4, :],
                        start=(r == 0 and k4 == 0), stop=(r == 7 and k4 == 3))) for k4 in range(4)],
                        reads=[("ring", s), pg(U_FT + 4096 * r, 4096)], writes=[BK(b)])
                ring_release()
                if nh == 1 and r == 3 and ci + 1 < NCHUNK:
                    S.phase = "N1"
                    rmsnorm_to_hT(ci + 1, C_GMIX, [0, 1, 2, 3])
                    S.phase = "MLP2"
            for tt in range(NT):
                b = 4 * nh + tt
                Xh = xbv[:, xbk, tt, nh * 512:(nh + 1) * 512]
                S.op("dve", lambda e, b=b, Xh=Xh: e.tensor_tensor(out=Xh, in0=banks[b][:], in1=Xh, op=ALU.add),
                     writes=[BK(b), ("x", xbk, tt)])
        S.phase = "FIN"
        for tt in range(NT):
            X = xbv[:, xbk, tt, :]
            sc = ssb[:, 8 + 4 * (tt % 2):8 + 4 * (tt % 2) + 4]
            sk = ("ssf", tt % 2)
            S.op("act", lambda e, X=X, sc=sc: e.activation(out=junk, in_=X, func=AF.Square, accum_out=sc[:, 0:1]),
                 reads=[("x", xbk, tt)], writes=[pg(O_JUNK, 2048), sk])
            S.op("act", lambda e, sc=sc: e.activation(out=sc[:, 1:2], in_=sc[:, 0:1], func=AF.Ln, scale=1.0 / D,
                                                      bias=col(C_EPS)), reads=["cols"], writes=[sk])
            S.op("act", lambda e, sc=sc: e.activation(out=sc[:, 2:3], in_=sc[:, 1:2], func=AF.Exp, scale=-0.5),
                 writes=[sk])
            S.op("dve", lambda e, X=X, sc=sc: e.scalar_tensor_tensor(out=X, in0=X, scalar=sc[:, 2:3], in1=gfin[:],
                                                                     op0=ALU.mult, op1=ALU.mult),
                 reads=[sk, "gfin"], writes=[("x", xbk, tt)])
        if ci == NCHUNK - 1:
            emit_store(ci)
    S._wait("sp", [(osem[b], S.dcnt.get(osem[b], 0)) for b in range(2) if S.dcnt.get(osem[b], 0) > 0])
    for cm in reversed(ctxs):
        cm.__exit__(None, None, None)
    S.close()
    nc._pe_labels = S.pe_labels
    return nc


_CACHE = {}


def kernel(**inputs):
    x = np.asarray(inputs["x"], np.float32)
    pos = np.asarray(inputs["positions"], np.int32)
    B, SL, _ = x.shape
    nseq = B // N_CORES
    nch = SL // T
    hp = host_prep(inputs)
    key = (nseq, nch)
    if key not in _CACHE:
        _CACHE[key] = build(nseq, nch)
    nc = _CACHE[key]
    in_maps = []
    for c in range(N_CORES):
        m = dict(hp)
        m["x"] = np.ascontiguousarray(x[c * nseq:(c + 1) * nseq].reshape(nseq * SL, D))
        m["pos"] = np.ascontiguousarray(pos[c * nseq:(c + 1) * nseq].reshape(1, nseq * SL))
        in_maps.append(m)
    res = run_bass_kernel_spmd(nc, in_maps, core_ids=list(range(N_CORES)))
    out = np.concatenate([np.asarray(r["out"], np.float32).reshape(nseq, SL, D) for r in res.results], axis=0)
    return out
```

```python
import math
import numpy as np
import concourse.bass as bass
import concourse.mybir as mybir
from concourse.bass_utils import run_bass_kernel_spmd

F32 = mybir.dt.float32
BF16 = mybir.dt.bfloat16
I32 = mybir.dt.int32
U8 = mybir.dt.uint8
ALU = mybir.AluOpType
AF = mybir.ActivationFunctionType

D = 1024
T = 512
NT = 4
NH = 8
DFF = 4096
EPS = 1e-6
SCALE = 1.0 / math.sqrt(192.0)
NSLOT = 5
PAGE = 1024
N_CORES = 8

C_GMIX, C_GMLP, C_PSC, C_GQ, C_GKV, C_BG, C_INVF, C_SGN, C_EPS, C_QUART, NCOL = 0, 8, 16, 20, 22, 23, 39, 40, 41, 42, 43


def _flat(keys):
    out = []
    for k in keys:
        if isinstance(k, list):
            out.extend(_flat(k))
        else:
            out.append(k)
    return out


class Sched:
    def __init__(self, nc):
        self.nc = nc
        self.eng = {"pe": nc.tensor, "dve": nc.vector, "act": nc.scalar, "pool": nc.gpsimd, "sp": nc.sync}
        self.semobj = {}
        self._ctx = []
        self.cnt = {}
        for k in self.eng:
            self.new_sem(k)
            self.cnt[k] = 0
        self.waited = {k: {} for k in self.eng}
        self.lastw = {}
        self.readers = {}
        self.dcnt = {}
        self.phase = "init"
        self.pe_labels = []

    def new_sem(self, name):
        cm = self.nc.semaphore("s_" + name)
        self.semobj[name] = cm.__enter__()
        self._ctx.append(cm)
        return name

    def close(self):
        for cm in reversed(self._ctx):
            cm.__exit__(None, None, None)

    def _deps(self, reads, writes):
        toks = []
        for k in reads:
            t = self.lastw.get(k)
            if t is not None:
                toks.append(t)
        for k in writes:
            t = self.lastw.get(k)
            if t is not None:
                toks.append(t)
            toks.extend(self.readers.get(k, ()))
        return toks

    def _pending(self, e, toks):
        need = {}
        for (s, v) in toks:
            if v > need.get(s, 0):
                need[s] = v
        w = self.waited[e]
        pend = []
        for s, v in need.items():
            if w.get(s, 0) < v:
                pend.append((s, v))
                w[s] = v
        return pend

    def _wait(self, e, toks):
        for s, v in self._pending(e, toks):
            self.eng[e].wait_ge(self.semobj[s], v)

    def _record(self, tok, reads, writes):
        for k in reads:
            self.readers.setdefault(k, []).append(tok)
        for k in writes:
            self.lastw[k] = tok
            self.readers[k] = []

    def op(self, e, fn, reads=(), writes=()):
        return self.group(e, [fn], reads, writes)

    def group(self, e, fns, reads=(), writes=()):
        if e == "pe":
            self.pe_labels.extend([self.phase] * len(fns))
        reads, writes = _flat(list(reads)), _flat(list(writes))
        pend = self._pending(e, self._deps(reads, writes))
        emb = pend.pop() if pend else None
        for s, v in pend:
            self.eng[e].wait_ge(self.semobj[s], v)
        ins = None
        for n, fn in enumerate(fns):
            ins = fn(self.eng[e])
            if n == 0 and emb is not None:
                ins._wait_ge(self.semobj[emb[0]], emb[1])
        self.cnt[e] += 1
        ins.then_inc(self.semobj[e], 1)
        tok = (e, self.cnt[e])
        self._record(tok, reads, writes)
        return tok

    def dma(self, e, semname, out, in_, reads=(), writes=()):
        reads, writes = _flat(list(reads)), _flat(list(writes))
        self._wait(e, self._deps(reads, writes))
        self.eng[e].dma_start(out=out, in_=in_).then_inc(self.semobj[semname], 16)
        self.dcnt[semname] = self.dcnt.get(semname, 0) + 16
        tok = (semname, self.dcnt[semname])
        self._record(tok, reads, writes)
        return tok

    def wait_keys(self, e, keys):
        keys = _flat(list(keys))
        toks = []
        for k in keys:
            t = self.lastw.get(k)
            if t is not None:
                toks.append(t)
            toks.extend(self.readers.get(k, ()))
        self._wait(e, toks)


def pg(off, nbytes):
    return [("pg", i) for i in range(off // PAGE, (off + nbytes - 1) // PAGE + 1)]


def piece_list():
    P = [("z", i) for i in range(5)]
    for mc in range(8):
        P.append(("g", mc))
    for q in range(4):
        P.append(("wvo", q))
    for nh in range(2):
        for r in range(2):
            P.append(("wo", nh, r))
    for q in range(16):
        P.append(("w1", q))
    for nh in range(2):
        for r in range(8):
            P.append(("w2", nh, r))
    return P


ZSETS = [[4, 5], [6, 7], [8], [0, 1], [2, 3]]
PIECES = piece_list()
NPIECE = len(PIECES)
HOST_PIECES = [i for i, p in enumerate(PIECES) if p[0] != "wvo"]
WVO_IDX = {p[1]: i for i, p in enumerate(PIECES) if p[0] == "wvo"}


def lhs_piece(W, colsets):
    K = W.shape[0]
    kc = K // 128
    out = np.zeros((128, 2048), np.float32)
    blocks = [W[:, cs].reshape(kc, 128, 128).transpose(1, 0, 2) for cs in colsets]
    arr = np.stack(blocks, axis=1).reshape(128, -1)
    out[:, : arr.shape[1]] = arr
    return out


def rhs_piece(W, rows0, col0):
    blk = W[rows0: rows0 + 512, col0: col0 + 512].reshape(4, 128, 512).transpose(1, 0, 2)
    return np.ascontiguousarray(blk).reshape(128, 2048)


def host_prep(inp):
    f = lambda a: np.asarray(a, np.float32)
    w_in = f(inp["w_in"])[0]
    ar = np.arange
    zc = [ar(0, 128), ar(128, 256), ar(256, 384), ar(384, 512),
          ar(512, 640), ar(640, 768),
          ar(768, 896),
          np.concatenate([ar(896, 960), ar(896, 960)]),
          np.concatenate([ar(928, 960), ar(896, 928), ar(928, 960), ar(896, 928)])]
    ga = [ar(960 + 128 * m, 960 + 128 * (m + 1)) for m in range(8)]
    gb = [ar(1984 + 128 * m, 1984 + 128 * (m + 1)) for m in range(8)]
    wpp = f(inp["w_pool_proj"])[0]
    w_out = f(inp["w_out"])[0]
    w1 = f(inp["w_mlp_in"])[0]
    w2 = f(inp["w_mlp_out"])[0]
    pieces = []
    for i in HOST_PIECES:
        p = PIECES[i]
        if p[0] == "z":
            pieces.append(lhs_piece(w_in, [zc[i] for i in ZSETS[p[1]]]))
        elif p[0] == "g":
            pieces.append(lhs_piece(w_in, [ga[p[1]], gb[p[1]]]))
        elif p[0] == "wo":
            pieces.append(rhs_piece(w_out, 512 * p[2], 512 * p[1]))
        elif p[0] == "w1":
            pieces.append(lhs_piece(w1, [ar(128 * (2 * p[1] + j), 128 * (2 * p[1] + j + 1)) for j in range(2)]))
        elif p[0] == "w2":
            pieces.append(rhs_piece(w2, 512 * p[2], 512 * p[1]))
    wst = np.stack(pieces, axis=0)
    wpph = np.concatenate([lhs_piece(wpp, [ar(128 * (4 * q + j), 128 * (4 * q + j + 1)) for j in range(4)])
                           for q in range(2)], axis=1)

    w_uq = f(inp["w_uq"])[0]
    w_ukv = f(inp["w_ukv"])[0]
    w_ap = f(inp["w_attn_proj"])[0]
    rope_cols, swap_cols = [], []
    for j in range(4):
        c, s = [], []
        for h in (2 * j, 2 * j + 1):
            b = h * 192 + 128
            c.append(ar(b, b + 64))
            s.append(np.concatenate([ar(b + 32, b + 64), ar(b, b + 32)]))
        rope_cols.append(np.concatenate(c))
        swap_cols.append(np.concatenate(s))
    wqr = lhs_piece(w_uq, rope_cols + swap_cols)
    A1 = np.stack([w_uq[:, h * 192: h * 192 + 128].T for h in range(8)], axis=1).reshape(128, 2048)
    A2 = np.stack([w_ukv[:, h * 256: h * 256 + 128].T for h in range(8)], axis=1).reshape(128, 1024)
    A3 = np.stack([w_ukv[:, h * 256 + 128: h * 256 + 256].T for h in range(8)], axis=1).reshape(128, 1024)
    A4 = np.ascontiguousarray(w_ap.reshape(8, 128, 1024).transpose(1, 0, 2)).reshape(128, 8192)
    poolw = np.ascontiguousarray(f(inp["pool_w"])[0].transpose(1, 0, 2)).reshape(128, 512)

    cols = np.zeros((128, NCOL), np.float32)
    cols[:, C_GMIX:C_GMIX + 8] = f(inp["g_mix"])[0].reshape(8, 128).T
    cols[:, C_GMLP:C_GMLP + 8] = f(inp["g_mlp"])[0].reshape(8, 128).T
    cols[:, C_PSC:C_PSC + 4] = f(inp["pool_scale"])[0].reshape(4, 128).T
    cols[:, C_GQ:C_GQ + 2] = f(inp["g_q"])[0].reshape(2, 128).T
    cols[:, C_GKV] = f(inp["g_kv"])[0]
    cols[:, C_BG:C_BG + 16] = f(inp["b_gate"])[0].reshape(16, 128).T
    half = 32
    inv_freq = (10000.0 ** (-np.arange(half, dtype=np.float32) / half)).astype(np.float32)
    p = np.arange(128)
    cols[:, C_INVF] = (inv_freq[p % 32].astype(np.float64) / (2 * np.pi)).astype(np.float32)
    twopi = 2 * np.pi * (1 - 1e-6)
    cols[:, C_SGN] = np.where((p % 64) < 32, -twopi, twopi).astype(np.float32)
    cols[:, C_EPS] = EPS
    cols[:, C_QUART] = 0.25
    ident = np.eye(128, dtype=np.float32)
    tri = (np.arange(128)[None, :] >= np.arange(128)[:, None]).astype(np.float32)
    invcnt = np.tile((1.0 / np.arange(1, 17, dtype=np.float32))[None, :], (128, 1)).astype(np.float32)
    return dict(wst=wst, wqr=wqr, A1=A1, A2=A2, A3=A3, A4=A4, poolw=poolw, cols=cols, ident=ident, tri=tri,
                invcnt=invcnt, wpph=wpph, gfin=f(inp["g_final"]).reshape(1, 1024))


def build(nseq=2, nch=8):
    S_LEN = nch * T
    NTOK = nseq * S_LEN
    NCHUNK = nseq * nch
    nc = bass.Bass("TRN2", target_bir_lowering=False)
    x_d = nc.dram_tensor("x", [NTOK, D], F32, kind="ExternalInput").ap()
    pos_d = nc.dram_tensor("pos", [1, NTOK], I32, kind="ExternalInput").ap()
    wst_d = nc.dram_tensor("wst", [len(HOST_PIECES), 128, 2048], F32, kind="ExternalInput").ap()
    wqr_d = nc.dram_tensor("wqr", [128, 2048], F32, kind="ExternalInput").ap()
    A1_d = nc.dram_tensor("A1", [128, 2048], F32, kind="ExternalInput").ap()
    A2_d = nc.dram_tensor("A2", [128, 1024], F32, kind="ExternalInput").ap()
    A3_d = nc.dram_tensor("A3", [128, 1024], F32, kind="ExternalInput").ap()
    A4_d = nc.dram_tensor("A4", [128, 8192], F32, kind="ExternalInput").ap()
    wpph_d = nc.dram_tensor("wpph", [128, 4096], F32, kind="ExternalInput").ap()
    poolw_d = nc.dram_tensor("poolw", [128, 512], F32, kind="ExternalInput").ap()
    cols_d = nc.dram_tensor("cols", [128, NCOL], F32, kind="ExternalInput").ap()
    ident_d = nc.dram_tensor("ident", [128, 128], F32, kind="ExternalInput").ap()
    tri_d = nc.dram_tensor("tri", [128, 128], F32, kind="ExternalInput").ap()
    invcnt_d = nc.dram_tensor("invcnt", [128, 16], F32, kind="ExternalInput").ap()
    gfin_d = nc.dram_tensor("gfin", [1, 1024], F32, kind="ExternalInput").ap()
    out_d = nc.dram_tensor("out", [NTOK, D], F32, kind="ExternalOutput").ap()
    wsc_d = nc.dram_tensor("wsc", [NPIECE, 128, 2048], BF16).ap()

    S = Sched(nc)
    U_UT, U_SAD, U_SBD, U_SAP, U_SBP = 0, 8448, 10560, 12672, 14784
    U_POOLED, U_CQ, U_CKV, U_KR, U_KRS, U_SQ, U_RSTD = 16896, 20992, 25088, 27136, 29184, 31232, 34304
    U_POSI, U_POSF, U_TS, U_TC = 38400, 40448, 42496, 44544
    U_SIZE = 47104
    U_FT, U_R = 0, 40448
    U_RT1, U_RT2 = U_SAD, U_SAD + 2048
    U_KT1, U_KT2 = U_SAP, U_SAP + 2048 + 64
    U_OT, U_YT, U_HS = 16896, 25088, 42496
    U_A4, U_STG, U_A1, U_A2, U_A3 = 0, 16384, 32768, 36864, 38912
    O_HT = U_SIZE
    O_MIX = O_HT + 8192
    O_CQN = O_MIX + 4096
    O_QABS = O_CQN + 2048
    O_QR = O_QABS + 8192
    O_PT = O_QR + 4096
    NPT = 6
    O_JUNK = O_PT + 1024 * NPT
    O_GT = O_JUNK + 2048
    A_SIZE = O_GT + 16384
    O_T1 = O_QABS
    U_ES, U_RS = 0, 4096
    U_ESP = 9216

    ctxs = []

    def sb(name, shape, dt):
        cm = nc.sbuf_tensor(name, shape, dt)
        t = cm.__enter__()
        ctxs.append(cm)
        return t

    def ps(name, shape, dt):
        cm = nc.psum_tensor(name, shape, dt)
        t = cm.__enter__()
        ctxs.append(cm)
        return t

    arena = sb("arena", [128, A_SIZE], U8)
    ring = sb("ring", [128, NSLOT * 2048], BF16)
    xb = sb("xb", [128, 2 * NT * D], F32)
    ckvnT = sb("ckvnT", [128, S_LEN], BF16)
    krlo = sb("krlo", [128, S_LEN], BF16)
    krhi = sb("krhi", [128, S_LEN], BF16)
    ckvtok = sb("ckvtok", [128, S_LEN], BF16)
    poolw = sb("poolw_s", [128, 512], BF16)
    wqr = sb("wqr_s", [128, 2048], BF16)
    wpps = sb("wpp_s", [128, 4096], BF16)
    wqa = sb("wqa_s", [128, 2048], BF16)
    identb = sb("identb", [128, 128], BF16)
    onesb = sb("onesb", [128, 128], BF16)
    onesf = sb("onesf", [128, 128], F32)
    trib = sb("trib", [128, 128], BF16)
    maskb = sb("maskb", [128, 128], BF16)
    cols = sb("cols_s", [128, NCOL], F32)
    bgh = sb("bgh", [128, 16], F32)
    invcnt = sb("invcnt_s", [128, 16], F32)
    gfin = sb("gfin_s", [128, 1024], F32)
    ssb = sb("ssb", [128, 48], F32)
    halo = sb("halo", [128, 64], F32)
    banks = [ps("bank%d" % i, [128, 512], F32) for i in range(8)]

    def AV(off, nbytes, dt):
        return arena[:, off:off + nbytes].bitcast(dt)

    def BK(b):
        return ("bank", b)

    uT = AV(U_UT, 8448, F32).rearrange("p (g t) -> p g t", g=4)
    sA = {"dve": AV(U_SAD, 2112, F32), "pool": AV(U_SAP, 2112, F32)}
    sB = {"dve": AV(U_SBD, 2112, F32), "pool": AV(U_SBP, 2112, F32)}
    sAk = {"dve": pg(U_SAD, 2112), "pool": pg(U_SAP, 2112)}
    sBk = {"dve": pg(U_SBD, 2112), "pool": pg(U_SBP, 2112)}
    pooledT = AV(U_POOLED, 4096, BF16).rearrange("p (g t) -> p g t", g=4)
    cq = AV(U_CQ, 4096, F32).rearrange("p (g t) -> p g t", g=2)
    ckv = AV(U_CKV, 2048, F32)
    kr = AV(U_KR, 2048, F32)
    krs = AV(U_KRS, 2048, F32)
    sq = AV(U_SQ, 3072, BF16).rearrange("p (g t) -> p g t", g=3)
    rstd = AV(U_RSTD, 4096, F32).rearrange("p (g t) -> p g t", g=2)
    posi = AV(U_POSI, 2048, I32)
    posf = AV(U_POSF, 2048, F32)
    ts_ = AV(U_TS, 2048, F32)
    tc_ = AV(U_TC, 2048, F32)
    fT = AV(U_FT, 32768, BF16).rearrange("p (g t) -> p g t", g=32)
    rbuf = [AV(U_R + 2048 * i, 2048, F32) for i in range(3)]
    rt1, rt2 = AV(U_RT1, 2048, F32), AV(U_RT2, 2048, F32)
    kt1, kt2 = AV(U_KT1, 2048, F32), AV(U_KT2, 2048, F32)
    oT = AV(U_OT, 8192, BF16).rearrange("p (g t) -> p g t", g=8)
    yT = AV(U_YT, 8192, BF16).rearrange("p (g t) -> p g t", g=8)
    HS_OFF = [43008, 45056, 40960, 34816]
    hs = [AV(o, 2048, BF16) for o in HS_OFF]
    hT = AV(O_HT, 8192, BF16).rearrange("p (g t) -> p g t", g=8)
    mixedT = AV(O_MIX, 4096, BF16).rearrange("p (g t) -> p g t", g=4)
    cqnT = AV(O_CQN, 2048, BF16).rearrange("p (g t) -> p g t", g=2)
    qabs = AV(O_QABS, 8192, BF16).rearrange("p (g t) -> p g t", g=8)
    qr = AV(O_QR, 4096, BF16).rearrange("p (g t) -> p g t", g=4)
    PT = [AV(O_PT + 1024 * i, 1024, BF16) for i in range(NPT)]
    rs = [AV(U_RS + 2048 * i, 2048, F32) for i in range(2)]
    esum = [AV(U_ES + 2048 * i, 2048, F32) for i in range(2)]
    esump = [AV(U_ESP + 2048 * i, 2048, F32) for i in range(2)]
    gts = AV(O_GT, 16384, BF16).rearrange("p (g t) -> p g t", g=16)
    junk = AV(O_JUNK, 2048, BF16)
    t12 = [[AV(O_T1 + 4096 * s + 2048 * i, 2048, F32) for i in range(2)] for s in range(2)]
    A4s = AV(U_A4, 16384, BF16).rearrange("p (h n) -> p h n", h=8)
    stg = AV(U_STG, 16384, BF16).rearrange("p (q j h m) -> p q j h m", q=4, j=2, h=8)
    A1s = AV(U_A1, 4096, BF16).rearrange("p (h r) -> p h r", h=8)
    A2s = AV(U_A2, 2048, BF16).rearrange("p (h c) -> p h c", h=8)
    A3s = AV(U_A3, 2048, BF16).rearrange("p (h c) -> p h c", h=8)
    xbv = xb[:].rearrange("p (b t d) -> p b t d", b=2, t=NT)
    ckvtokv = ckvtok[:].rearrange("p (b c) -> p b c", c=128)
    wqrv = wqr[:].rearrange("p (j k m) -> p j k m", j=8, k=2)
    wqav = wqa[:].rearrange("p (k h c) -> p k h c", k=2, h=8)
    poolwv = poolw[:].rearrange("p (g d) -> p g d", g=4)

    def col(i):
        return cols[:, i:i + 1]

    xsem = [S.new_sem("x%d" % b) for b in range(2)]
    osem = [S.new_sem("o%d" % b) for b in range(2)]
    psem = S.new_sem("pos")

    def load_x(ci):
        b = ci % 2
        S.dma("sp", xsem[b], xbv[:, b], x_d[ci * T:(ci + 1) * T, :].rearrange("(t p) d -> p t d", p=128),
              writes=[("x", b, tt) for tt in range(NT)])

    def load_pos(ci):
        S.dma("sp", psem, posi, pos_d[0:1, ci * T:(ci + 1) * T].partition_broadcast(128), writes=pg(U_POSI, 2048))

    load_x(0)
    if NCHUNK > 1:
        load_x(1)
    isem = [S.new_sem("i%d" % i) for i in range(16)]
    S.dma("sp", isem[0], cols[:], cols_d[:, :], writes=["cols"])
    S.dma("sp", isem[1], invcnt[:], invcnt_d[:, :], writes=["invcnt"])
    S.dma("sp", isem[2], gfin[:], gfin_d[0:1, :].partition_broadcast(128), writes=["gfin"])
    S.dma("pool", isem[3], identb[:], ident_d[:, :], writes=["ident"])
    S.dma("pool", isem[4], trib[:], tri_d[:, :], writes=["tri"])
    S.dma("pool", isem[5], poolw[:], poolw_d[:, :], writes=["poolw"])
    S.dma("pool", isem[6], wqr[:], wqr_d[:, :], writes=["wqr"])
    for q in range(2):
        S.dma("pool", isem[14 + q], wpps[:, 2048 * q:2048 * (q + 1)], wpph_d[:, 2048 * q:2048 * (q + 1)],
              writes=[("wpp", q)])
    S.dma("pool", isem[7], AV(U_A1, 4096, BF16), A1_d[:, :], writes=pg(U_A1, 4096))
    S.dma("pool", isem[8], AV(U_A2, 2048, BF16), A2_d[:, :], writes=pg(U_A2, 2048))
    S.dma("pool", isem[9], AV(U_A3, 2048, BF16), A3_d[:, :], writes=pg(U_A3, 2048))
    for i in range(4):
        S.dma("pool", isem[10 + i], AV(U_A4 + 4096 * i, 4096, BF16), A4_d[:, 2048 * i:2048 * (i + 1)],
              writes=pg(U_A4 + 4096 * i, 4096))
    for n, i in enumerate(HOST_PIECES):
        sname = S.new_sem("c%d" % i)
        S.dma("pool", sname, wsc_d[i], wst_d[n], reads=([("wsc", 12)] if n > 12 else []), writes=[("wsc", i)])
    S.op("dve", lambda e: e.memset(onesb[:], 1.0), writes=["ones"])
    S.op("dve", lambda e: e.tensor_scalar(out=maskb[:], in0=trib[:], scalar1=-1.0, scalar2=30000.0, op0=ALU.add,
                                           op1=ALU.mult), reads=["tri"], writes=["maskb"])
    S.op("dve", lambda e: e.memset(onesf[:], 1.0), writes=["onesf"])
    S.op("dve", lambda e: e.memset(krlo[:], 0.0), writes=[("krlo", c) for c in range(nch)])
    S.op("dve", lambda e: e.memset(krhi[:], 0.0), writes=[("krhi", c) for c in range(nch)])
    S.op("dve", lambda e: e.tensor_scalar(out=bgh[:], in0=cols[:, C_BG:C_BG + 16], scalar1=0.5, scalar2=None,
                                           op0=ALU.mult), reads=["cols"], writes=["bgh"])
    bctr = [0]

    def nb():
        b = bctr[0] % 8
        bctr[0] += 1
        return b

    for rc in range(2):
        for hg in range(2):
            b = nb()
            S.group("pe", [(lambda e, i=i, b=b, rc=rc, hg=hg: e.matmul(
                out=banks[b][:, i * 128:(i + 1) * 128], lhsT=A1s[:, hg * 4 + i, rc * 128:(rc + 1) * 128],
                rhs=A2s[:, hg * 4 + i, :], start=True, stop=True)) for i in range(4)],
                reads=[pg(U_A1, 4096), pg(U_A2, 2048)], writes=[BK(b)])
            S.op("dve", lambda e, b=b, rc=rc, hg=hg: e.tensor_copy(
                out=wqav[:, rc, hg * 4:(hg + 1) * 4, :],
                in_=banks[b][:].rearrange("p (h c) -> p h c", h=4)), writes=[BK(b), "wqa"])
    for h in range(8):
        for nh in range(2):
            b = nb()
            S.group("pe", [lambda e, b=b, h=h, nh=nh: e.matmul(out=banks[b][:], lhsT=A3s[:, h, :],
                                                                rhs=A4s[:, h, nh * 512:(nh + 1) * 512],
                                                                start=True, stop=True)],
                    reads=[pg(U_A3, 2048), pg(U_A4, 16384)], writes=[BK(b)])
            eng = "dve" if (h + nh) % 2 == 0 else "act"
            if eng == "dve":
                S.op("dve", lambda e, b=b, h=h, nh=nh: e.tensor_copy(
                    out=stg[:, 2 * nh:2 * nh + 2, :, h, :],
                    in_=banks[b][:].rearrange("p (q j m) -> p q j m", q=2, j=2)),
                    writes=[BK(b), pg(U_STG + 8192 * nh, 8192)])
            else:
                S.op("act", lambda e, b=b, h=h, nh=nh: e.activation(
                    out=stg[:, 2 * nh:2 * nh + 2, :, h, :],
                    in_=banks[b][:].rearrange("p (q j m) -> p q j m", q=2, j=2), func=AF.Copy),
                    writes=[BK(b), pg(U_STG + 8192 * nh, 8192)])
    for q in range(4):
        sname = S.new_sem("v%d" % q)
        S.dma("sp", sname, wsc_d[WVO_IDX[q]], AV(U_STG + 4096 * q, 4096, BF16),
              reads=pg(U_STG + 4096 * q, 4096), writes=[("wsc", WVO_IDX[q])])

    rsem = [S.new_sem("r%d" % s) for s in range(NSLOT)]
    rst = {"issued": 0, "acq": 0, "rel": 0}
    TOTAL = NCHUNK * NPIECE

    def ring_prefetch():
        while rst["issued"] < min(TOTAL, rst["rel"] + NSLOT):
            g = rst["issued"]
            s = g % NSLOT
            p = g % NPIECE
            S.dma("sp", rsem[s], ring[:, s * 2048:(s + 1) * 2048], wsc_d[p], reads=[("wsc", p)], writes=[("ring", s)])
            rst["issued"] += 1

    def ring_acquire(kind):
        g = rst["acq"]
        assert PIECES[g % NPIECE][0] == kind, (PIECES[g % NPIECE], kind)
        ring_prefetch()
        assert rst["issued"] > g
        rst["acq"] += 1
        s = g % NSLOT
        return s, ring[:, s * 2048:(s + 1) * 2048]

    def ring_release():
        rst["rel"] += 1
        ring_prefetch()

    def rmsnorm_to_hT(ci, gcol, tb):
        b = ci % 2
        for tt in range(NT):
            X = xbv[:, b, tt, :]
            sc = ssb[:, 32 + 4 * tt:32 + 4 * tt + 4]
            sk = ("ss", tt)
            S.op("act", lambda e, X=X, sc=sc: e.activation(out=junk, in_=X, func=AF.Square, accum_out=sc[:, 0:1]),
                 reads=[("x", b, tt)], writes=[pg(O_JUNK, 2048), sk])
            S.op("act", lambda e, sc=sc: e.activation(out=sc[:, 1:2], in_=sc[:, 0:1], func=AF.Ln, scale=1.0 / D,
                                                      bias=col(C_EPS)), reads=["cols"], writes=[sk])
            S.op("act", lambda e, sc=sc: e.activation(out=sc[:, 2:3], in_=sc[:, 1:2], func=AF.Exp, scale=-0.5),
                 writes=[sk])
            h_ = hs[tt]
            hk = pg(HS_OFF[tt], 2048)
            S.op("dve", lambda e, X=X, sc=sc, h_=h_: e.tensor_scalar(out=h_, in0=X, scalar1=sc[:, 2:3], scalar2=None,
                                                                      op0=ALU.mult),
                 reads=[("x", b, tt), sk], writes=[hk])
            S.group("pe", [(lambda e, kc=kc, h_=h_, tt=tt: e.transpose(
                out=banks[tb[kc // 2]][:].bitcast(BF16)[:, (kc % 2) * 512 + tt * 128:(kc % 2) * 512 + (tt + 1) * 128],
                in_=h_[:, kc * 128:(kc + 1) * 128], identity=identb[:])) for kc in range(8)],
                reads=[hk, "ident"], writes=[BK(tb[i]) for i in range(4)])
        for kc in range(8):
            src = banks[tb[kc // 2]][:].bitcast(BF16)[:, (kc % 2) * 512:(kc % 2) * 512 + 512]
            if (kc // 2) % 2 == 0:
                S.op("act", lambda e, kc=kc, src=src: e.activation(out=hT[:, kc, :], in_=src, func=AF.Copy,
                                                                   scale=col(gcol + kc)),
                     reads=["cols"], writes=[BK(tb[kc // 2]), pg(O_HT + 1024 * kc, 1024)])
            else:
                S.op("dve", lambda e, kc=kc, src=src: e.tensor_scalar(out=hT[:, kc, :], in0=src, scalar1=col(gcol + kc),
                                                                      scalar2=None, op0=ALU.mult),
                     reads=["cols"], writes=[BK(tb[kc // 2]), pg(O_HT + 1024 * kc, 1024)])

    U_HS2 = 0
    hs2 = [AV(U_HS2 + 2048 * i, 2048, BF16) for i in range(4)]

    def n2_stats(ci, tt):
        b = ci % 2
        X = xbv[:, b, tt, :]
        sc = ssb[:, 16 + 4 * tt:16 + 4 * tt + 4]
        sk = ("ss2", tt)
        S.op("act", lambda e, X=X, sc=sc: e.activation(out=junk, in_=X, func=AF.Square, accum_out=sc[:, 0:1]),
             reads=[("x", b, tt)], writes=[pg(O_JUNK, 2048), sk])
        S.op("act", lambda e, sc=sc: e.activation(out=sc[:, 1:2], in_=sc[:, 0:1], func=AF.Ln, scale=1.0 / D,
                                                  bias=col(C_EPS)), reads=["cols"], writes=[sk])
        S.op("act", lambda e, sc=sc: e.activation(out=sc[:, 2:3], in_=sc[:, 1:2], func=AF.Exp, scale=-0.5),
             writes=[sk])

    def n2_scale(ci, tt):
        b = ci % 2
        X = xbv[:, b, tt, :]
        sc = ssb[:, 16 + 4 * tt:16 + 4 * tt + 4]
        S.op("dve", lambda e, X=X, sc=sc, h_=hs2[tt]: e.tensor_scalar(out=h_, in0=X, scalar1=sc[:, 2:3], scalar2=None,
                                                                      op0=ALU.mult),
             reads=[("x", b, tt), ("ss2", tt)], writes=[pg(U_HS2 + 2048 * tt, 2048)])

    def n2_transposes(tt):
        h_ = hs2[tt]
        S.group("pe", [(lambda e, kc=kc, h_=h_, tt=tt: e.transpose(
            out=banks[kc // 2][:].bitcast(BF16)[:, (kc % 2) * 512 + tt * 128:(kc % 2) * 512 + (tt + 1) * 128],
            in_=h_[:, kc * 128:(kc + 1) * 128], identity=identb[:])) for kc in range(8)],
            reads=[pg(U_HS2 + 2048 * tt, 2048), "ident"], writes=[BK(i) for i in range(4)])

    def n2_evacs(gcol):
        for kc in range(8):
            src = banks[kc // 2][:].bitcast(BF16)[:, (kc % 2) * 512:(kc % 2) * 512 + 512]
            if kc in (0, 3, 6):
                S.op("act", lambda e, kc=kc, src=src: e.activation(out=hT[:, kc, :], in_=src, func=AF.Copy,
                                                                   scale=col(gcol + kc)),
                     reads=["cols"], writes=[BK(kc // 2), pg(O_HT + 1024 * kc, 1024)])
            else:
                S.op("dve", lambda e, kc=kc, src=src: e.tensor_scalar(out=hT[:, kc, :], in0=src, scalar1=col(gcol + kc),
                                                                      scalar2=None, op0=ALU.mult),
                     reads=["cols"], writes=[BK(kc // 2), pg(O_HT + 1024 * kc, 1024)])

    HTK = pg(O_HT, 8192)
    evt = [0]

    def evac_copy(dst, dkeys, b, scale_ap=None, extra_reads=()):
        evt[0] += 1
        if evt[0] % 2 == 0:
            if scale_ap is None:
                S.op("act", lambda e: e.activation(out=dst, in_=banks[b][:], func=AF.Copy), reads=list(extra_reads),
                     writes=[BK(b), dkeys])
            else:
                S.op("act", lambda e: e.activation(out=dst, in_=banks[b][:], func=AF.Copy, scale=scale_ap),
                     reads=list(extra_reads), writes=[BK(b), dkeys])
        else:
            if scale_ap is None:
                S.op("dve", lambda e: e.tensor_copy(out=dst, in_=banks[b][:]), reads=list(extra_reads),
                     writes=[BK(b), dkeys])
            else:
                S.op("dve", lambda e: e.tensor_scalar(out=dst, in0=banks[b][:], scalar1=scale_ap, scalar2=None,
                                                      op0=ALU.mult), reads=list(extra_reads), writes=[BK(b), dkeys])

    zdst = [(uT[:, g, 16:528], pg(U_UT + 2112 * g, 2112)) for g in range(4)] + \
           [(cq[:, 0, :], pg(U_CQ, 2048)), (cq[:, 1, :], pg(U_CQ + 2048, 2048)),
            (ckv, pg(U_CKV, 2048)), (kr, pg(U_KR, 2048)), (krs, pg(U_KRS, 2048))]

    def z_piece(zp):
        s, rp = ring_acquire("z")
        rv = rp.rearrange("p (j k m) -> p j k m", j=2, k=8)
        for j, zi in enumerate(ZSETS[zp]):
            b = nb()
            S.group("pe", [(lambda e, kc=kc, j=j, rv=rv, b=b: e.matmul(out=banks[b][:], lhsT=rv[:, j, kc, :],
                                                                       rhs=hT[:, kc, :], start=(kc == 0),
                                                                       stop=(kc == 7))) for kc in range(8)],
                    reads=[("ring", s), HTK], writes=[BK(b)])
            evac_copy(zdst[zi][0], zdst[zi][1], b)
        ring_release()

    def pool_group(c, g):
        e_ = "dve" if g < 2 else "pool"
        w = 2 ** (g + 1)
        uk = pg(U_UT + 2112 * g, 2112)
        src, srck = uT[:, g, :], uk
        bufs = [(sA[e_], sAk[e_]), (sB[e_], sBk[e_])]
        for k in range(g + 1):
            sh = 2 ** k
            lo = 2 ** (k + 1) - 1
            dst, dstk = bufs[k % 2]
            S.op(e_, lambda e, src=src, dst=dst, sh=sh, lo=lo: e.tensor_tensor(
                out=dst[:, lo:528], in0=src[:, lo:528], in1=src[:, lo - sh:528 - sh], op=ALU.add),
                reads=[srck], writes=[dstk])
            src, srck = dst, dstk
        S.op("dve", lambda e, src=src, g=g, w=w: e.scalar_tensor_tensor(
            out=pooledT[:, g, :], in0=src[:, 16:528], scalar=1.0 / w, in1=uT[:, g, 16:528],
            op0=ALU.mult, op1=ALU.subtract), reads=[srck, uk], writes=pg(U_POOLED + 1024 * g, 1024))
        if c == 0:
            other = bufs[(g + 1) % 2]
            S.op(e_, lambda e, src=src, other=other, w=w: e.tensor_tensor(
                out=other[0][:, 0:w - 1], in0=src[:, 16:16 + w - 1], in1=invcnt[:, 0:w - 1], op=ALU.mult),
                reads=[srck, "invcnt"], writes=[other[1]])
            S.op(e_, lambda e, other=other, g=g, w=w: e.tensor_tensor(
                out=pooledT[:, g, 0:w - 1], in0=other[0][:, 0:w - 1], in1=uT[:, g, 16:16 + w - 1],
                op=ALU.subtract), reads=[other[1], uk], writes=pg(U_POOLED + 1024 * g, 1024))
        S.op(e_, lambda e, g=g: e.tensor_copy(out=halo[:, 16 * g:16 * (g + 1)], in_=uT[:, g, 512:528]),
             reads=[uk], writes=[("halo", g)])

    def pool_linear(g):
        b = nb()
        S.group("pe", [lambda e, g=g, b=b: e.matmul(out=banks[b][:], lhsT=poolwv[:, g, :], rhs=pooledT[:, g, :],
                                                    start=True, stop=True)],
                reads=["poolw", pg(U_POOLED + 1024 * g, 1024)], writes=[BK(b)])
        evac_copy(mixedT[:, g, :], pg(O_MIX + 1024 * g, 1024), b, scale_ap=col(C_PSC + g), extra_reads=["cols"])

    def gate_pre(mc):
        s, rp = ring_acquire("g")
        gv = rp.rearrange("p (j k m) -> p j k m", j=2, k=8)
        for i in range(2):
            b = nb()
            S.group("pe", [(lambda e, kc=kc, i=i, b=b: e.matmul(out=banks[b][:], lhsT=gv[:, i, kc, :],
                                                                rhs=hT[:, kc, :], start=(kc == 0), stop=(kc == 7)))
                           for kc in range(8)], reads=[("ring", s), HTK], writes=[BK(b)])
            S.op("act", lambda e, i=i, b=b: e.activation(out=gts[:, 8 * i + mc, :], in_=banks[b][:], func=AF.Tanh,
                                                         scale=0.5, bias=bgh[:, 8 * i + mc:8 * i + mc + 1]),
                 reads=["bgh"], writes=[BK(b), pg(O_GT + 1024 * (8 * i + mc), 1024)])
        ring_release()

    SINK, COSK = pg(U_TS, 2048), pg(U_TC, 2048)
    CQNK = pg(O_CQN, 2048)

    def stream_A(ci):
        c = ci % nch
        S.op("act", lambda e: e.activation(out=sq[:, 0:2, :], in_=cq[:, :, :], func=AF.Square),
             reads=pg(U_CQ, 4096), writes=pg(U_SQ, 2048))
        S.op("act", lambda e: e.activation(out=sq[:, 2, :], in_=ckv, func=AF.Square),
             reads=pg(U_CKV, 2048), writes=pg(U_SQ + 2048, 1024))
        S.op("dve", lambda e: e.tensor_copy(out=posf, in_=posi), reads=pg(U_POSI, 2048), writes=pg(U_POSF, 2048))
        S.op("pool", lambda e: e.tensor_scalar(out=ts_, in0=posf, scalar1=col(C_INVF), scalar2=0.0, op0=ALU.mult,
                                               op1=ALU.add),
             reads=[pg(U_POSF, 2048), "cols"], writes=pg(U_TS, 2048))
        S.op("pool", lambda e: e.tensor_scalar(out=tc_, in0=ts_, scalar1=1.0, scalar2=0.25, op0=ALU.mult,
                                               op1=ALU.add),
             reads=pg(U_TS, 2048), writes=pg(U_TC, 2048))
        yield
        bq, bkv = nb(), nb()
        S.group("pe", [(lambda e, i=i: e.matmul(out=banks[bq][:], lhsT=onesb[:], rhs=sq[:, i, :], start=(i == 0),
                                                stop=(i == 1))) for i in range(2)],
                reads=["ones", pg(U_SQ, 2048)], writes=[BK(bq)])
        S.group("pe", [lambda e: e.matmul(out=banks[bkv][:], lhsT=onesb[:], rhs=sq[:, 2, :], start=True, stop=True)],
                reads=["ones", pg(U_SQ + 2048, 1024)], writes=[BK(bkv)])
        for i, (bb, n) in enumerate(((bq, 256), (bkv, 128))):
            rk = pg(U_RSTD + 2048 * i, 2048)
            S.op("act", lambda e, i=i, bb=bb, n=n: e.activation(out=rstd[:, i, :], in_=banks[bb][:], func=AF.Ln,
                                                                scale=1.0 / n, bias=col(C_EPS)),
                 reads=["cols"], writes=[BK(bb), rk])
            S.op("act", lambda e, i=i: e.activation(out=rstd[:, i, :], in_=rstd[:, i, :], func=AF.Exp, scale=-0.5),
                 writes=[rk])
        for (tv, tk) in ((ts_, pg(U_TS, 2048)), (tc_, pg(U_TC, 2048))):
            S.op("dve", lambda e, tv=tv: e.tensor_copy(out=posi, in_=tv), reads=[tk], writes=pg(U_POSI, 2048))
            S.op("dve", lambda e: e.tensor_copy(out=posf, in_=posi), reads=pg(U_POSI, 2048), writes=pg(U_POSF, 2048))
            S.op("pool", lambda e, tv=tv: e.tensor_tensor(out=tv, in0=tv, in1=posf, op=ALU.subtract),
                 reads=pg(U_POSF, 2048), writes=[tk])
        yield
        for rc in range(2):
            S.op("dve", lambda e, rc=rc: e.scalar_tensor_tensor(out=cqnT[:, rc, :], in0=cq[:, rc, :],
                                                                scalar=col(C_GQ + rc), in1=rstd[:, 0, :],
                                                                op0=ALU.mult, op1=ALU.mult),
                 reads=[pg(U_CQ + 2048 * rc, 2048), pg(U_RSTD, 2048), "cols"], writes=pg(O_CQN + 1024 * rc, 1024))
        S.op("dve", lambda e: e.scalar_tensor_tensor(out=ckvnT[:, c * T:(c + 1) * T], in0=ckv, scalar=col(C_GKV),
                                                     in1=rstd[:, 1, :], op0=ALU.mult, op1=ALU.mult),
             reads=[pg(U_CKV, 2048), pg(U_RSTD + 2048, 2048), "cols"], writes=[("ckvnT", c)])
        S.op("act", lambda e: e.activation(out=ts_, in_=ts_, func=AF.Sin, scale=col(C_SGN)), reads=["cols"],
             writes=pg(U_TS, 2048))
        S.op("act", lambda e: e.activation(out=tc_, in_=tc_, func=AF.Sin, scale=float(2 * np.pi * (1 - 1e-6))),
             writes=pg(U_TC, 2048))
        yield
        bt = nb()
        S.group("pe", [(lambda e, i=i: e.transpose(out=banks[bt][:].bitcast(BF16)[:, i * 128:(i + 1) * 128],
                                                   in_=ckvnT[:, c * T + i * 128:c * T + (i + 1) * 128],
                                                   identity=identb[:])) for i in range(4)],
                reads=[("ckvnT", c), "ident"], writes=[BK(bt)])
        S.op("dve", lambda e: e.tensor_copy(out=ckvtok[:, c * T:(c + 1) * T], in_=banks[bt][:].bitcast(BF16)[:, 0:512]),
             writes=[BK(bt), ("ckvtok", c)])
        S.op("dve", lambda e: e.tensor_tensor(out=kt1, in0=kr, in1=tc_, op=ALU.mult), reads=[pg(U_KR, 2048), COSK],
             writes=pg(U_KT1, 2048))
        S.op("pool", lambda e: e.tensor_tensor(out=kt2, in0=krs, in1=ts_, op=ALU.mult), reads=[pg(U_KRS, 2048), SINK],
             writes=pg(U_KT2, 2048))
        S.op("dve", lambda e: e.tensor_tensor(out=krlo[0:64, c * T:(c + 1) * T], in0=kt1[0:64, :], in1=kt2[0:64, :],
                                              op=ALU.add), reads=[pg(U_KT1, 2048), pg(U_KT2, 2048)],
             writes=[("krlo", c)])
        S.op("dve", lambda e: e.tensor_tensor(out=krhi[64:128, c * T:(c + 1) * T], in0=kt1[64:128, :],
                                              in1=kt2[64:128, :], op=ALU.add),
             reads=[pg(U_KT1, 2048), pg(U_KT2, 2048)], writes=[("krhi", c)])
        for h in range(NH):
            b = nb()
            S.group("pe", [(lambda e, rc=rc, h=h, b=b: e.matmul(out=banks[b][:], lhsT=wqav[:, rc, h, :],
                                                                rhs=cqnT[:, rc, :], start=(rc == 0), stop=(rc == 1)))
                           for rc in range(2)], reads=["wqa", CQNK], writes=[BK(b)])
            evac_copy(qabs[:, h, :], pg(O_QABS + 1024 * h, 1024), b)
            if h == 3:
                yield
        yield
        for j in range(4):
            b1, b2 = nb(), nb()
            S.group("pe", [(lambda e, rc=rc, j=j: e.matmul(out=banks[b1][:], lhsT=wqrv[:, j, rc, :], rhs=cqnT[:, rc, :],
                                                           start=(rc == 0), stop=(rc == 1))) for rc in range(2)],
                    reads=["wqr", CQNK], writes=[BK(b1)])
            S.group("pe", [(lambda e, rc=rc, j=j: e.matmul(out=banks[b2][:], lhsT=wqrv[:, 4 + j, rc, :],
                                                           rhs=cqnT[:, rc, :], start=(rc == 0), stop=(rc == 1)))
                           for rc in range(2)], reads=["wqr", CQNK], writes=[BK(b2)])
            S.op("dve", lambda e, b1=b1: e.tensor_tensor(out=rt1, in0=banks[b1][:], in1=tc_, op=ALU.mult),
                 reads=[COSK], writes=[BK(b1), pg(U_RT1, 2048)])
            S.op("dve", lambda e, b2=b2: e.tensor_tensor(out=rt2, in0=banks[b2][:], in1=ts_, op=ALU.mult),
                 reads=[SINK], writes=[BK(b2), pg(U_RT2, 2048)])
            S.op("pool", lambda e, j=j: e.tensor_tensor(out=qr[:, j, :], in0=rt1, in1=rt2, op=ALU.add),
                 reads=[pg(U_RT1, 2048), pg(U_RT2, 2048)], writes=pg(O_QR + 1024 * j, 1024))
            if j % 2 == 1:
                yield

    def stream_B(ci):
        c = ci % nch
        z_piece(3)
        z_piece(4)
        pool_group(c, 2)
        pool_group(c, 3)
        yield
        gate_pre(0)
        gate_pre(1)
        yield
        gate_pre(2)
        gate_pre(3)
        gate_pre(4)
        pool_group(c, 0)
        pool_group(c, 1)
        yield
        gate_pre(5)
        yield
        gate_pre(6)
        yield
        gate_pre(7)
        yield
        for g in range(4):
            pool_linear(g)
        yield

    SBANKS = [0, 1, 2, 7]
    LOOK = 2

    def attention(ci):
        c = ci % nch
        nkb = 4 * c + 4
        n_o = 4 * c
        korder = []
        for i in range(4):
            korder.extend(range((i * n_o) // 4, ((i + 1) * n_o) // 4))
            korder.append(n_o + i)
        assert sorted(korder) == list(range(nkb)) and korder[0] == 0
        units = [(h, kb) for h in range(NH) for kb in korder]
        KFIRST, KLAST = korder[0], korder[-1]
        NU = len(units)

        def emit_S(u):
            h, kb = units[u]
            off = max(0, kb - 4 * c) * 128
            sbk = SBANKS[u % 4]
            krc = krlo if h % 2 == 0 else krhi
            krk = ("krlo" if h % 2 == 0 else "krhi", kb // 4)
            fns = [lambda e: e.matmul(out=banks[sbk][:, off:512], lhsT=ckvnT[:, kb * 128:(kb + 1) * 128],
                                      rhs=qabs[:, h, off:512], start=True, stop=False)]
            if kb >= 4 * c:
                fns.append(lambda e: e.matmul(out=banks[sbk][:, off:off + 128], lhsT=identb[:], rhs=maskb[:],
                                              start=False, stop=False))
            fns.append(lambda e: e.matmul(out=banks[sbk][:, off:512], lhsT=krc[:, kb * 128:(kb + 1) * 128],
                                          rhs=qr[:, h // 2, off:512], start=False, stop=True))
            S.group("pe", fns,
                    reads=[("ckvnT", kb // 4), krk, pg(O_QABS + 1024 * h, 1024), pg(O_QR + 1024 * (h // 2), 1024),
                           "ident", "maskb"],
                    writes=[BK(sbk)])

        D1, D2 = min(6, nkb - 3), min(8, nkb - 1)
        pend = []
        PESUM = 10
        pesum_n = [0] * NH

        def fin_pe(h):
            sb_ = 5 + (h % 2)
            es = esum[h % 2]
            esk = pg(U_ES + 2048 * (h % 2), 2048)
            S.phase = "ATTF"
            first = pesum_n[h] == 0
            S.group("pe", [lambda e: e.matmul(out=banks[sb_][:], lhsT=onesf[:], rhs=es, start=first, stop=True)],
                    reads=["onesf", esk], writes=[BK(sb_)])
            S.phase = "ATT"

        def fin_rest(h):
            ob, sb_ = 3 + (h % 2), 5 + (h % 2)
            r_ = rs[h % 2]
            rk = pg(U_RS + 2048 * (h % 2), 2048)
            S.op("act", lambda e: e.activation(out=r_, in_=banks[sb_][:], func=AF.Ln), writes=[BK(sb_), rk])
            S.op("act", lambda e: e.activation(out=r_, in_=r_, func=AF.Exp, scale=-1.0), writes=[rk])
            S.op("dve", lambda e: e.tensor_tensor(out=oT[:, h, :], in0=banks[ob][:], in1=r_, op=ALU.mult),
                 reads=[rk], writes=[BK(ob), pg(U_OT + 1024 * h, 1024)])

        for u in range(min(LOOK, NU)):
            emit_S(u)
        for u in range(NU):
            h, kb = units[u]
            off = max(0, kb - 4 * c) * 128
            if u + LOOK < NU:
                emit_S(u + LOOK)
            sbk = SBANKS[u % 4]
            pt = PT[u % NPT]
            ptk = pg(O_PT + 1024 * (u % NPT), 1024)
            S.op("act", lambda e, sbk=sbk, pt=pt, off=off: e.activation(out=pt[:, off:512], in_=banks[sbk][:, off:512],
                                                                        func=AF.Exp, scale=SCALE),
                 writes=[BK(sbk), ptk])
            ob, sb_ = 3 + (h % 2), 5 + (h % 2)
            es = esum[h % 2]
            esk = pg(U_ES + 2048 * (h % 2), 2048)
            S.group("pe", [
                lambda e, ob=ob, pt=pt, off=off, kb=kb: e.matmul(out=banks[ob][:, off:512], lhsT=ckvtokv[:, kb, :],
                                                                 rhs=pt[:, off:512], start=(kb == KFIRST),
                                                                 stop=(kb == KLAST))],
                reads=[("ckvtok", kb // 4), ptk], writes=[BK(ob)])
            if kb < 4 * c and kb % PESUM == PESUM - 1:
                first = pesum_n[h] == 0
                pesum_n[h] += 1
                S.group("pe", [lambda e, sb_=sb_, pt=pt, first=first: e.matmul(out=banks[sb_][:], lhsT=onesb[:], rhs=pt,
                                                                               start=first, stop=False)],
                        reads=["ones", ptk], writes=[BK(sb_)])
            elif kb == KFIRST:
                S.op("dve", lambda e, es=es, pt=pt: e.tensor_copy(out=es, in_=pt), reads=[ptk], writes=[esk])
            else:
                S.op("dve", lambda e, es=es, pt=pt, off=off: e.tensor_tensor(out=es[:, off:512], in0=es[:, off:512],
                                                                            in1=pt[:, off:512], op=ALU.add),
                     reads=[ptk], writes=[esk])
            for p in list(pend):
                if p[2] == 0 and u - p[1] >= D1:
                    fin_pe(p[0])
                    p[2] = 1
                elif p[2] == 1 and u - p[1] >= D2:
                    fin_rest(p[0])
                    pend.remove(p)
            if kb == KLAST:
                for p in list(pend):
                    if p[2] == 0:
                        fin_pe(p[0])
                    fin_rest(p[0])
                    pend.remove(p)
                pend.append([h, u, 0])
        for p in list(pend):
            if p[2] == 0:
                fin_pe(p[0])
            fin_rest(p[0])

    def gate_phase(ci):
        MIXK, OTK = pg(O_MIX, 4096), pg(U_OT, 8192)
        wvo_s = None
        AAB, BMB = [0, 2, 7, 1], [3, 4, 5, 6]

        def aa(mc):
            wppv = wpps[:, 2048 * (mc // 4):2048 * (mc // 4 + 1)].rearrange("p (j k m) -> p j k m", j=4, k=4)
            b = AAB[mc % 4]
            S.group("pe", [(lambda e, kc=kc: e.matmul(out=banks[b][:], lhsT=wppv[:, mc % 4, kc, :],
                                                      rhs=mixedT[:, kc, :], start=(kc == 0), stop=(kc == 3)))
                           for kc in range(4)], reads=[("wpp", mc // 4), MIXK], writes=[BK(b)])

        for mc in range(4):
            aa(mc)
        for mc in range(8):
            if mc % 2 == 0:
                wvo_s = ring_acquire("wvo")
            wvov = wvo_s[1].rearrange("p (j h m) -> p j h m", j=2, h=8)
            bs = [AAB[mc % 4], BMB[mc % 4]]
            S.group("pe", [(lambda e, h=h: e.matmul(out=banks[bs[1]][:], lhsT=wvov[:, mc % 2, h, :], rhs=oT[:, h, :],
                                                    start=(h == 0), stop=(h == 7))) for h in range(8)],
                    reads=[("ring", wvo_s[0]), OTK], writes=[BK(bs[1])])
            st = mc % 2
            for i in range(2):
                S.op("dve", lambda e, i=i: e.scalar_tensor_tensor(out=t12[st][i], in0=gts[:, 8 * i + mc, :], scalar=1.0,
                                                                  in1=banks[bs[i]][:], op0=ALU.add, op1=ALU.mult),
                     reads=pg(O_GT + 1024 * (8 * i + mc), 1024),
                     writes=[BK(bs[i]), pg(O_T1 + 4096 * st + 2048 * i, 2048)])
            S.op("pool", lambda e: e.tensor_tensor(out=yT[:, mc, :], in0=t12[st][0], in1=t12[st][1], op=ALU.add),
                 reads=pg(O_T1 + 4096 * st, 4096), writes=pg(U_YT + 1024 * mc, 1024))
            if mc % 2 == 1:
                ring_release()
            if mc + 4 < 8:
                aa(mc + 4)

    def emit_store(ci):
        b = ci % 2
        S.dma("sp", osem[b], out_d[ci * T:(ci + 1) * T, :].rearrange("(t p) d -> p t d", p=128), xbv[:, b],
              reads=[("x", b, tt) for tt in range(NT)])

    load_pos(0)
    S.phase = "N1"
    rmsnorm_to_hT(0, C_GMIX, [nb() for _ in range(4)])
    for ci in range(NCHUNK):
        c = ci % nch
        xbk = ci % 2
        XK = [("x", xbk, tt) for tt in range(NT)]
        S.phase = "Z"
        bctr[0] = 0
        if c == 0:
            S.op("pool", lambda e: e.memset(uT[:, :, 0:16], 0.0), writes=pg(U_UT, 8448))
        else:
            S.op("pool", lambda e: e.tensor_copy(out=uT[:, :, 0:16], in_=halo[:].rearrange("p (g t) -> p g t", g=4)),
                 reads=[("halo", g) for g in range(4)], writes=pg(U_UT, 8448))
        for zp in range(3):
            z_piece(zp)
        S.phase = "MIX"
        gens = [stream_A(ci), stream_B(ci)]
        alive = [True, True]
        while any(alive):
            for gi in range(2):
                if alive[gi]:
                    try:
                        next(gens[gi])
                    except StopIteration:
                        alive[gi] = False
        if ci >= 1:
            emit_store(ci - 1)
            if ci + 1 < NCHUNK:
                load_x(ci + 1)
        if ci + 1 < NCHUNK:
            load_pos(ci + 1)
        S.phase = "ATT"
        attention(ci)
        S.phase = "GATE"
        gate_phase(ci)
        S.phase = "WOUT"
        YTK = pg(U_YT, 8192)
        for nh in range(2):
            pcs = [ring_acquire("wo"), ring_acquire("wo")]
            def wo_mm(r, tt):
                pv = pcs[r][1].rearrange("p (k n) -> p k n", k=4)
                b = 4 * nh + tt
                S.group("pe", [(lambda e, k4=k4, pv=pv, b=b, tt=tt, r=r: e.matmul(
                    out=banks[b][:], lhsT=yT[:, 4 * r + k4, tt * 128:(tt + 1) * 128], rhs=pv[:, k4, :],
                    start=(r == 0 and k4 == 0), stop=(r == 1 and k4 == 3))) for k4 in range(4)],
                    reads=[("ring", pcs[r][0]), pg(U_YT + 4096 * r, 4096)], writes=[BK(b)])

            def wo_add(tt):
                b = 4 * nh + tt
                Xh = xbv[:, xbk, tt, nh * 512:(nh + 1) * 512]
                S.op("dve", lambda e, b=b, Xh=Xh: e.scalar_tensor_tensor(out=Xh, in0=banks[b][:], scalar=0.5, in1=Xh,
                                                                         op0=ALU.mult, op1=ALU.add),
                     writes=[BK(b), ("x", xbk, tt)])

            if nh == 0:
                for r in range(2):
                    for tt in range(NT):
                        wo_mm(r, tt)
                ring_release()
                ring_release()
                for tt in range(NT):
                    wo_add(tt)
            else:
                for tt in range(NT):
                    S.phase = "WOUT"
                    for r in range(2):
                        wo_mm(r, tt)
                    wo_add(tt)
                    S.phase = "N2"
                    n2_stats(ci, tt)
                    if tt >= 1:
                        n2_scale(ci, tt - 1)
                    if tt >= 2:
                        n2_transposes(tt - 2)
                ring_release()
                ring_release()
                n2_scale(ci, NT - 1)
                n2_transposes(NT - 2)
                n2_transposes(NT - 1)
                n2_evacs(C_GMLP)
        S.phase = "MLP1"
        bctr[0] = 4
        for q in range(16):
            s, rp = ring_acquire("w1")
            rv = rp.rearrange("p (j k m) -> p j k m", j=2, k=8)
            for j in range(2):
                fc = 2 * q + j
                b = nb()
                if fc == 0:
                    for kc in range(8):
                        S.group("pe", [lambda e, kc=kc, j=j, rv=rv, b=b: e.matmul(
                            out=banks[b][:], lhsT=rv[:, j, kc, :], rhs=hT[:, kc, :], start=(kc == 0), stop=(kc == 7))],
                            reads=[("ring", s), pg(O_HT + 1024 * kc, 1024)], writes=[BK(b)])
                else:
                    S.group("pe", [(lambda e, kc=kc, j=j, rv=rv, b=b: e.matmul(out=banks[b][:], lhsT=rv[:, j, kc, :],
                                                                               rhs=hT[:, kc, :], start=(kc == 0),
                                                                               stop=(kc == 7))) for kc in range(8)],
                            reads=[("ring", s), HTK], writes=[BK(b)])
                rb = rbuf[fc % 3]
                rk = pg(U_R + 2048 * (fc % 3), 2048)
                S.op("act", lambda e, b=b, rb=rb: e.activation(out=rb, in_=banks[b][:], func=AF.Relu),
                     writes=[BK(b), rk])
                e2 = "pool" if fc % 3 == 2 else "dve"
                S.op(e2, lambda e, rb=rb, fc=fc: e.tensor_tensor(out=fT[:, fc, :], in0=rb, in1=rb, op=ALU.mult),
                     reads=[rk], writes=pg(U_FT + 1024 * fc, 1024))
            ring_release()
        S.phase = "MLP2"
        for nh in range(2):
            for r in range(8):
                s, rp = ring_acquire("w2")
                pv = rp.rearrange("p (k n) -> p k n", k=4)
                for tt in range(NT):
                    b = 4 * nh + tt
                    S.group("pe", [(lambda e, k4=k4, pv=pv, b=b, tt=tt, r=r: e.matmul(
                        out=banks[b][:], lhsT=fT[:, 4 * r + k4, tt * 128:(tt + 1) * 128], rhs=pv[:, k4, :],
                        start=(r == 0 and k4 == 0), stop=(r == 7 and k4 == 3))) for k4 in range(4)],
                        reads=[("ring", s), pg(U_FT + 4096 * r, 4096)], writes=[BK(b)])
                ring_release()
                if nh == 1 and r == 3 and ci + 1 < NCHUNK:
                    S.phase = "N1"
                    rmsnorm_to_hT(ci + 1, C_GMIX, [0, 1, 2, 3])
                    S.phase = "MLP2"
            for tt in range(NT):
                b = 4 * nh + tt
                Xh = xbv[:, xbk, tt, nh * 512:(nh + 1) * 512]
                S.op("dve", lambda e, b=b, Xh=Xh: e.tensor_tensor(out=Xh, in0=banks[b][:], in1=Xh, op=ALU.add),
                     writes=[BK(b), ("x", xbk, tt)])
        S.phase = "FIN"
        for tt in range(NT):
            X = xbv[:, xbk, tt, :]
            sc = ssb[:, 8 + 4 * (tt % 2):8 + 4 * (tt % 2) + 4]
            sk = ("ssf", tt % 2)
            S.op("act", lambda e, X=X, sc=sc: e.activation(out=junk, in_=X, func=AF.Square, accum_out=sc[:, 0:1]),
                 reads=[("x", xbk, tt)], writes=[pg(O_JUNK, 2048), sk])
            S.op("act", lambda e, sc=sc: e.activation(out=sc[:, 1:2], in_=sc[:, 0:1], func=AF.Ln, scale=1.0 / D,
                                                      bias=col(C_EPS)), reads=["cols"], writes=[sk])
            S.op("act", lambda e, sc=sc: e.activation(out=sc[:, 2:3], in_=sc[:, 1:2], func=AF.Exp, scale=-0.5),
                 writes=[sk])
            S.op("dve", lambda e, X=X, sc=sc: e.scalar_tensor_tensor(out=X, in0=X, scalar=sc[:, 2:3], in1=gfin[:],
                                                                     op0=ALU.mult, op1=ALU.mult),
                 reads=[sk, "gfin"], writes=[("x", xbk, tt)])
        if ci == NCHUNK - 1:
            emit_store(ci)
    S._wait("sp", [(osem[b], S.dcnt.get(osem[b], 0)) for b in range(2) if S.dcnt.get(osem[b], 0) > 0])
    for cm in reversed(ctxs):
        cm.__exit__(None, None, None)
    S.close()
    nc._pe_labels = S.pe_labels
    return nc


_CACHE = {}


def kernel(**inputs):
    x = np.asarray(inputs["x"], np.float32)
    pos = np.asarray(inputs["positions"], np.int32)
    B, SL, _ = x.shape
    nseq = B // N_CORES
    nch = SL // T
    hp = host_prep(inputs)
    key = (nseq, nch)
    if key not in _CACHE:
        _CACHE[key] = build(nseq, nch)
    nc = _CACHE[key]
    in_maps = []
    for c in range(N_CORES):
        m = dict(hp)
        m["x"] = np.ascontiguousarray(x[c * nseq:(c + 1) * nseq].reshape(nseq * SL, D))
        m["pos"] = np.ascontiguousarray(pos[c * nseq:(c + 1) * nseq].reshape(1, nseq * SL))
        in_maps.append(m)
    res = run_bass_kernel_spmd(nc, in_maps, core_ids=list(range(N_CORES)))
    out = np.concatenate([np.asarray(r["out"], np.float32).reshape(nseq, SL, D) for r in res.results], axis=0)
    return out
```
